# Optimizing a Trainium2 kernel written in Bass

```python
import jax, jax.numpy as jnp
from jax import lax
import numpy as np

D_MODEL = 1024
BATCH = 16
SEQ = 4096
DEPTH = 1

CHUNK = 64
N_PAST_CHUNKS = 8
BAND_CHUNKS = N_PAST_CHUNKS + 1
D_RNN = 1024
RNN_BLOCKS = 8
RNN_BLOCK_DIM = D_RNN // RNN_BLOCKS
CONV_WIDTH = 4
LRU_C = 8.0
ATT_HEADS = 8
ATT_HEAD_DIM = 128
D_ATT = ATT_HEADS * ATT_HEAD_DIM
MAX_REL = 256
MEM_TOKENS = 256
MEM_HEADS = 4
MEM_HEAD_DIM = 256
D_MEM = MEM_HEADS * MEM_HEAD_DIM
N_BRANCHES = 3
D_IN = 2 * D_RNN + 4 * D_ATT + 2 * D_MEM + N_BRANCHES * D_MODEL
EPS = 1e-6
NEG_INF = -1e30

kernel_name = "hybrid_rglru_chunkattn_memxattn_gated"


def _rmsnorm(x, g):
    xf = x.astype(jnp.float32)
    y = xf * lax.rsqrt(jnp.mean(xf * xf, axis=-1, keepdims=True) + EPS)
    return (y * g.astype(jnp.float32)).astype(x.dtype)


def _split_columns(u):
    sizes = (D_RNN, D_RNN, D_ATT, D_ATT, D_ATT, D_ATT, D_MEM, D_MEM, N_BRANCHES * D_MODEL)
    points = [int(p) for p in np.cumsum(sizes)[:-1]]
    return jnp.split(u, points, axis=-1)


def _rglru_branch(xr, conv_w, conv_b, wa, ba, wx, bx, lam):
    B, S, _ = xr.shape
    xc = lax.conv_general_dilated(
        xr, conv_w[:, None, :], window_strides=(1,), padding=[(CONV_WIDTH - 1, 0)],
        dimension_numbers=("NWC", "WIO", "NWC"), feature_group_count=D_RNN) + conv_b
    xb = xc.reshape(B, S, RNN_BLOCKS, RNN_BLOCK_DIM)
    r = jax.nn.sigmoid(jnp.einsum("bsni,nij->bsnj", xb, wa).reshape(B, S, D_RNN) + ba)
    i = jax.nn.sigmoid(jnp.einsum("bsni,nij->bsnj", xb, wx).reshape(B, S, D_RNN) + bx)
    log_a = -LRU_C * r.astype(jnp.float32) * jax.nn.softplus(-lam.astype(jnp.float32))
    a = jnp.exp(log_a)
    b = jnp.sqrt(-jnp.expm1(2.0 * log_a)) * (i * xc).astype(jnp.float32)

    def combine(left, right):
        a1, b1 = left
        a2, b2 = right
        return a1 * a2, a2 * b1 + b2

    _, h = lax.associative_scan(combine, (a, b), axis=1)
    return h.astype(xr.dtype)


def _chunk_band_attention(q, k, v, q_norm_g, k_norm_g, rel_bias):
    B, S, _ = q.shape
    n_chunks = S // CHUNK
    past = N_PAST_CHUNKS * CHUNK
    band = BAND_CHUNKS * CHUNK
    q = _rmsnorm(q.reshape(B, S, ATT_HEADS, ATT_HEAD_DIM), q_norm_g)
    k = _rmsnorm(k.reshape(B, S, ATT_HEADS, ATT_HEAD_DIM), k_norm_g)
    v = v.reshape(B, S, ATT_HEADS, ATT_HEAD_DIM)
    kp = jnp.pad(k, ((0, 0), (past, 0), (0, 0), (0, 0)))
    vp = jnp.pad(v, ((0, 0), (past, 0), (0, 0), (0, 0)))
    dist = jnp.arange(CHUNK)[:, None] + past - jnp.arange(band)[None, :]
    bias = rel_bias[:, jnp.clip(dist, -MAX_REL, MAX_REL) + MAX_REL].astype(jnp.float32)
    scale = ATT_HEAD_DIM ** -0.5

    def one_chunk(c):
        start = c * CHUNK
        qc = lax.dynamic_slice_in_dim(q, start, CHUNK, axis=1)
        kc = lax.dynamic_slice_in_dim(kp, start, band, axis=1)
        vc = lax.dynamic_slice_in_dim(vp, start, band, axis=1)
        s = jnp.einsum("bqhd,bkhd->bhqk", qc, kc).astype(jnp.float32) * scale + bias
        kpos = start - past + jnp.arange(band)
        s = jnp.where(kpos[None, None, None, :] >= 0, s, NEG_INF)
        p = jax.nn.softmax(s, axis=-1).astype(vc.dtype)
        return jnp.einsum("bhqk,bkhd->bqhd", p, vc)

    o = lax.map(one_chunk, jnp.arange(n_chunks))
    return o.transpose(1, 0, 2, 3, 4).reshape(B, S, D_ATT)


def _memory_attention(qm, mem_n, w_mem_kv, q_norm_g, k_norm_g):
    B, S, _ = qm.shape
    M = mem_n.shape[1]
    q = _rmsnorm(qm.reshape(B, S, MEM_HEADS, MEM_HEAD_DIM), q_norm_g)
    km, vm = jnp.split(mem_n @ w_mem_kv, 2, axis=-1)
    k = _rmsnorm(km.reshape(B, M, MEM_HEADS, MEM_HEAD_DIM), k_norm_g)
    v = vm.reshape(B, M, MEM_HEADS, MEM_HEAD_DIM)
    s = jnp.einsum("bshd,bmhd->bhsm", q, k).astype(jnp.float32) * (MEM_HEAD_DIM ** -0.5)
    p = jax.nn.softmax(s, axis=-1).astype(v.dtype)
    return jnp.einsum("bhsm,bmhd->bshd", p, v).reshape(B, S, D_MEM)


def _layer(x, mem, norm_g, mem_norm_g, w_in, b_merge, conv_w, conv_b, lru_wa, lru_ba,
           lru_wx, lru_bx, lru_lambda, q_norm_g, k_norm_g, rel_bias, w_mem_kv,
           mem_q_norm_g, mem_k_norm_g, w_proj_rnn, w_proj_att, w_proj_mem, w_out):
    B, S, _ = x.shape
    h = _rmsnorm(x, norm_g)
    xr, gr, q, k, v, ga, qm, gm, gmerge = _split_columns(h @ w_in)
    y_rnn = (_rglru_branch(xr, conv_w, conv_b, lru_wa, lru_ba, lru_wx, lru_bx, lru_lambda)
             * jax.nn.silu(gr)) @ w_proj_rnn
    y_att = (_chunk_band_attention(q, k, v, q_norm_g, k_norm_g, rel_bias)
             * jax.nn.silu(ga)) @ w_proj_att
    y_mem = (_memory_attention(qm, _rmsnorm(mem, mem_norm_g), w_mem_kv, mem_q_norm_g, mem_k_norm_g)
             * jax.nn.silu(gm)) @ w_proj_mem
    g = jax.nn.sigmoid(gmerge + b_merge).reshape(B, S, N_BRANCHES, D_MODEL)
    y = g[:, :, 0] * y_rnn + g[:, :, 1] * y_att + g[:, :, 2] * y_mem
    return x + y @ w_out


def setup_inputs(seed: int = 0) -> dict:
    key = jax.random.key(seed)
    ks = jax.random.split(key, 24)
    f32 = jnp.float32

    def nrm(k, shape, scale):
        return jax.random.normal(k, (DEPTH,) + shape, f32) * scale

    a8 = jax.random.uniform(ks[12], (DEPTH, D_RNN), f32, 0.9, 0.999)
    a = a8 ** (1.0 / LRU_C)
    lru_lambda = jnp.log(a) - jnp.log1p(-a)
    return {
        "x": jax.random.normal(ks[0], (BATCH, SEQ, D_MODEL), f32),
        "mem": jax.random.normal(ks[1], (BATCH, MEM_TOKENS, D_MODEL), f32),
        "norm_g": 1.0 + nrm(ks[2], (D_MODEL,), 0.05),
        "mem_norm_g": 1.0 + nrm(ks[3], (D_MODEL,), 0.05),
        "w_in": nrm(ks[4], (D_MODEL, D_IN), D_MODEL ** -0.5),
        "b_merge": nrm(ks[5], (N_BRANCHES * D_MODEL,), 0.01),
        "conv_w": nrm(ks[6], (CONV_WIDTH, D_RNN), CONV_WIDTH ** -0.5),
        "conv_b": nrm(ks[7], (D_RNN,), 0.01),
        "lru_wa": nrm(ks[8], (RNN_BLOCKS, RNN_BLOCK_DIM, RNN_BLOCK_DIM), RNN_BLOCK_DIM ** -0.5),
        "lru_ba": nrm(ks[9], (D_RNN,), 0.01),
        "lru_wx": nrm(ks[10], (RNN_BLOCKS, RNN_BLOCK_DIM, RNN_BLOCK_DIM), RNN_BLOCK_DIM ** -0.5),
        "lru_bx": nrm(ks[11], (D_RNN,), 0.01),
        "lru_lambda": lru_lambda,
        "q_norm_g": 1.0 + nrm(ks[13], (ATT_HEAD_DIM,), 0.05),
        "k_norm_g": 1.0 + nrm(ks[14], (ATT_HEAD_DIM,), 0.05),
        "rel_bias": nrm(ks[15], (ATT_HEADS, 2 * MAX_REL + 1), 0.1),
        "w_mem_kv": nrm(ks[16], (D_MODEL, 2 * D_MEM), D_MODEL ** -0.5),
        "mem_q_norm_g": 1.0 + nrm(ks[17], (MEM_HEAD_DIM,), 0.05),
        "mem_k_norm_g": 1.0 + nrm(ks[18], (MEM_HEAD_DIM,), 0.05),
        "w_proj_rnn": nrm(ks[19], (D_RNN, D_MODEL), D_RNN ** -0.5),
        "w_proj_att": nrm(ks[20], (D_ATT, D_MODEL), D_ATT ** -0.5),
        "w_proj_mem": nrm(ks[21], (D_MEM, D_MODEL), D_MEM ** -0.5),
        "w_out": nrm(ks[22], (D_MODEL, D_MODEL), D_MODEL ** -0.5),
    }


def reference(x, mem, norm_g, mem_norm_g, w_in, b_merge, conv_w, conv_b, lru_wa, lru_ba,
              lru_wx, lru_bx, lru_lambda, q_norm_g, k_norm_g, rel_bias, w_mem_kv,
              mem_q_norm_g, mem_k_norm_g, w_proj_rnn, w_proj_att, w_proj_mem, w_out):
    for l in range(DEPTH):
        x = _layer(x, mem, norm_g[l], mem_norm_g[l], w_in[l], b_merge[l], conv_w[l], conv_b[l],
                   lru_wa[l], lru_ba[l], lru_wx[l], lru_bx[l], lru_lambda[l], q_norm_g[l],
                   k_norm_g[l], rel_bias[l], w_mem_kv[l], mem_q_norm_g[l], mem_k_norm_g[l],
                   w_proj_rnn[l], w_proj_att[l], w_proj_mem[l], w_out[l])
    return x
```

```python
import contextlib
import numpy as np
import concourse.bass as bass
import concourse.mybir as mybir
from concourse.bass_utils import run_bass_kernel_spmd

F32 = mybir.dt.float32
BF16 = mybir.dt.bfloat16
AF = mybir.ActivationFunctionType
ALU = mybir.AluOpType
AX = mybir.AxisListType

N_CORES = 8
SEQ = 4096
DM = 1024
TT = 512
NT = SEQ // TT
NSEQ = 2
MEMT = 256
EPS = 1e-6
NSLOT = 4

C_XR, C_GR, C_Q, C_K, C_V, C_GA, C_QM, C_GM, C_MG = 0, 1024, 2048, 3072, 4096, 5120, 6144, 7168, 8192


class Buf:
    __slots__ = ("name", "last_write", "reads")

    def __init__(self, name=""):
        self.name = name
        self.last_write = None
        self.reads = []


class DSem:
    def __init__(self, sem):
        self.sem = sem
        self.count = 0


class Instr:
    __slots__ = ("fn", "waits", "signal", "dsem", "val")

    def __init__(self, fn, waits, dsem=None):
        self.fn = fn
        self.waits = waits
        self.signal = False
        self.dsem = dsem
        self.val = 0


class FW:
    ENGS = ("pe", "act", "dve", "pool", "sp")

    def __init__(self, nc, stack, tag=""):
        self.nc = nc
        self.stack = stack
        self.tag = tag
        self.instrs = {e: [] for e in self.ENGS}
        self.sems = {e: stack.enter_context(nc.semaphore("s_" + tag + e)) for e in self.ENGS}
        self.seen = {e: {} for e in self.ENGS}
        self.seen_d = {e: {} for e in self.ENGS}
        self.dsems = []

    def dsem(self, name):
        s = self.stack.enter_context(self.nc.semaphore(self.tag + name))
        d = DSem(s)
        self.dsems.append(d)
        return d

    def _deps(self, eng, reads, writes):
        deps = []
        for b in reads:
            if b.last_write is not None:
                deps.append((b.last_write, "raw"))
        for b in writes:
            if b.last_write is not None:
                deps.append((b.last_write, "waw"))
            for r in b.reads:
                deps.append((r, "war"))
        best = {}
        out = []
        for d, kind in deps:
            if d[0] == "c":
                _, e2, idx = d
                if e2 == eng and (kind != "raw" or eng == "pe"):
                    continue
                if self.seen[eng].get(e2, -1) >= idx:
                    continue
                best[e2] = max(best.get(e2, -1), idx)
            else:
                _, ds, cnt = d
                if self.seen_d[eng].get(id(ds), 0) >= cnt:
                    continue
                self.seen_d[eng][id(ds)] = cnt
                out.append(("d", ds, cnt))
        for e2, idx in best.items():
            self.seen[eng][e2] = idx
            out.append(("c", e2, idx))
        return out

    def op(self, eng, fn, R=(), W=()):
        waits = self._deps(eng, R, W)
        idx = len(self.instrs[eng])
        self.instrs[eng].append(Instr(fn, waits))
        tag = ("c", eng, idx)
        for b in R:
            b.reads.append(tag)
        for b in W:
            b.last_write = tag
            b.reads = []
        return tag

    def dma(self, fn, dsem, R=(), W=(), eng="sp"):
        waits = self._deps(eng, R, W)
        dsem.count += 1
        ins = Instr(fn, waits, dsem=dsem)
        ins.val = dsem.count
        self.instrs[eng].append(ins)
        tag = ("d", dsem, dsem.count)
        for b in R:
            b.reads.append(tag)
        for b in W:
            b.last_write = tag
            b.reads = []
        return tag

    def barrier(self):
        last = {}
        for e2 in self.ENGS:
            idx = len(self.instrs[e2]) - 1
            while idx >= 0 and (self.instrs[e2][idx].dsem is not None or self.instrs[e2][idx].fn is None):
                idx -= 1
            last[e2] = idx
        for e in self.ENGS:
            waits = []
            for e2 in self.ENGS:
                idx = last[e2]
                if idx >= 0 and e2 != e and self.seen[e].get(e2, -1) < idx:
                    self.seen[e][e2] = idx
                    waits.append(("c", e2, idx))
            for ds in self.dsems:
                if ds.count > 0 and self.seen_d[e].get(id(ds), 0) < ds.count:
                    self.seen_d[e][id(ds)] = ds.count
                    waits.append(("d", ds, ds.count))
            if waits:
                self.instrs[e].append(Instr(None, waits))

    def wait_all_dma(self, eng="sp"):
        waits = []
        for ds in self.dsems:
            if ds.count > 0 and self.seen_d[eng].get(id(ds), 0) < ds.count:
                self.seen_d[eng][id(ds)] = ds.count
                waits.append(("d", ds, ds.count))
        if waits:
            self.instrs[eng].append(Instr(None, waits))

    def emit(self):
        nc = self.nc
        for e in self.ENGS:
            for ins in self.instrs[e]:
                for w in ins.waits:
                    if w[0] == "c":
                        self.instrs[w[1]][w[2]].signal = True
        for e in self.ENGS:
            c = 0
            for ins in self.instrs[e]:
                if ins.dsem is None and ins.signal:
                    c += 1
                    ins.val = c

        def replay(ename, h):
            for ins in self.instrs[ename]:
                for w in ins.waits:
                    if w[0] == "c":
                        h.wait_ge(self.sems[w[1]], self.instrs[w[1]][w[2]].val)
                    else:
                        h.wait_ge(w[1].sem, 16 * w[2])
                if ins.fn is None:
                    continue
                bi = ins.fn(h)
                if ins.dsem is not None:
                    bi.then_inc(ins.dsem.sem, 16)
                elif ins.signal:
                    bi.then_inc(self.sems[ename], 1)

        with nc.Block() as block:
            @block.tensor
            def _(h):
                replay("pe", h)

            @block.scalar
            def _(h):
                replay("act", h)

            @block.vector
            def _(h):
                replay("dve", h)

            @block.gpsimd
            def _(h):
                replay("pool", h)

            @block.sync
            def _(h):
                replay("sp", h)


def unit_table():
    units = []

    def add(name, src, c0, rs):
        units.append((name, src, c0, rs))
        return len(units) - 1

    U = {}
    for hh in range(2):
        U["MK%d" % hh] = add("MK%d" % hh, "w_mem_kv", hh * 512, "mng")
    for hh in range(2):
        U["MV%d" % hh] = add("MV%d" % hh, "w_mem_kv", 1024 + hh * 512, "mng")
    for nm, c in (("GR", C_GR), ("XR", C_XR), ("Q", C_Q), ("K", C_K), ("V", C_V), ("GA", C_GA), ("QM", C_QM), ("GM", C_GM)):
        for hh in range(2):
            U["%s%d" % (nm, hh)] = add("%s%d" % (nm, hh), "w_in", c + hh * 512, "ng")
    for b in range(3):
        for hh in range(2):
            U["MG%d_%d" % (b, hh)] = add("MG%d_%d" % (b, hh), "w_in", C_MG + b * 1024 + hh * 512, "ng")
    for b, src in enumerate(("w_proj_rnn", "w_proj_att", "w_proj_mem")):
        for hh in range(2):
            U["P%d_%d" % (b, hh)] = add("P%d_%d" % (b, hh), src, hh * 512, None)
    for hh in range(2):
        U["WO%d" % hh] = add("WO%d" % hh, "w_out", hh * 512, 0.5)
    return units, U


def tile_stream(U):
    s = ["GR0", "GR1", "XR0", "Q0", "Q1", "K0", "K1", "XR1", "V0", "V1", "GA0", "GA1", "MG0_0", "P0_0", "MG0_1", "P0_1",
         "MG1_0", "P1_0", "MG1_1", "P1_1",
         "QM0", "QM1", "GM0", "GM1", "MG2_0", "P2_0", "MG2_1", "P2_1", "WO0", "WO1"]
    return [U[k] for k in s]


CP_NG, CP_MNG, CP_CW, CP_CB, CP_BA, CP_BX, CP_LAM, CP_BM, CP_QG, CP_KG, CP_MQG, CP_MKG, NCP = 0, 8, 16, 48, 56, 64, 72, 80, 104, 105, 106, 108, 112
DC_HBA, DC_HBX, DC_HBM, DC_CF, DC_CH, DC_MH, DC_T0, DC_T1, DC_T2, DC_T3, DC_EPS, NDC = 0, 8, 16, 40, 48, 56, 57, 65, 73, 81, 89, 96


def build_program(nseq=NSEQ, nt=NT, level=99):
    nc = bass.Bass("TRN2", target_bir_lowering=False)
    units, U = unit_table()
    NU = len(units)
    dr = {}
    dr["x"] = nc.dram_tensor("x", [NSEQ * SEQ, DM], F32, kind="ExternalInput").ap()
    dr["mem"] = nc.dram_tensor("mem", [NSEQ * MEMT, DM], F32, kind="ExternalInput").ap()
    dr["w_in"] = nc.dram_tensor("w_in", [DM, 11264], F32, kind="ExternalInput").ap()
    dr["w_mem_kv"] = nc.dram_tensor("w_mem_kv", [DM, 2048], F32, kind="ExternalInput").ap()
    for k in ("w_proj_rnn", "w_proj_att", "w_proj_mem", "w_out"):
        dr[k] = nc.dram_tensor(k, [DM, DM], F32, kind="ExternalInput").ap()
    dr["lru_wa"] = nc.dram_tensor("lru_wa", [8, 128, 128], F32, kind="ExternalInput").ap()
    dr["lru_wx"] = nc.dram_tensor("lru_wx", [8, 128, 128], F32, kind="ExternalInput").ap()
    dr["cpack"] = nc.dram_tensor("cpack", [128, NCP], F32, kind="ExternalInput").ap()
    dr["biasr"] = nc.dram_tensor("biasr", [128, 8, 640], F32, kind="ExternalInput").ap()
    out = nc.dram_tensor("out", [NSEQ * SEQ, DM], F32, kind="ExternalOutput").ap()
    scr = nc.dram_tensor("scr", [NU, 128, 4096], BF16, kind="Internal").ap()

    with contextlib.ExitStack() as st0:
        def T0(name, shape, dt):
            return st0.enter_context(nc.sbuf_tensor(name, shape, dt))

        cp = T0("cp", [128, NCP], F32)
        dc = T0("dc", [128, NDC], F32)
        ident = T0("ident", [128, 128], BF16)
        ones = T0("ones", [128, 128], BF16)
        wab = T0("wab", [128, 8, 128], BF16)
        wxb = T0("wxb", [128, 8, 128], BF16)
        biasR = T0("biasR", [128, 8, 640], BF16)
        B_scr = [Buf("scr%d" % i) for i in range(NU)]

        with contextlib.ExitStack() as st1:
            fw = FW(nc, st0, "a")

            def T1(name, shape, dt):
                return st1.enter_context(nc.sbuf_tensor(name, shape, dt))

            NSB = 4
            stage = [T1("stage%d" % i, [128, 8, 512], F32) for i in range(NSB)]
            cvt = [T1("cvt%d" % i, [128, 8, 512], BF16) for i in range(NSB)]
            identf = T1("identf", [128, 128], F32)
            B_stage = [Buf() for _ in range(NSB)]
            B_cvt = [Buf() for _ in range(NSB)]
            B_c = Buf("consts")
            d_stage = [fw.dsem("dst%d" % i) for i in range(NSB)]
            d_cvt = [fw.dsem("dcv%d" % i) for i in range(NSB)]
            d_c = fw.dsem("dc")

            fw.dma(lambda e: e.dma_start(out=cp[:], in_=dr["cpack"]), d_c, W=[B_c])
            fw.op("pool", lambda e: e.memset(identf[:], 0.0), W=[B_c])
            fw.op("pool", lambda e: e.affine_select(out=identf[:], in_=identf[:], compare_op=ALU.not_equal, fill=1.0,
                                                    base=0, pattern=[[-1, 128]], channel_multiplier=1), R=[B_c], W=[B_c])
            fw.op("pool", lambda e: e.tensor_copy(out=ident[:], in_=identf[:]), R=[B_c], W=[B_c])
            fw.op("pool", lambda e: e.memset(ones[:], 1.0), W=[B_c])
            fw.op("pool", lambda e: e.memset(dc[:, DC_MH:DC_MH + 1], -0.5), W=[B_c])
            fw.op("dve", lambda e: e.tensor_scalar(out=dc[:, DC_HBA:DC_HBA + 16], in0=cp[:, CP_BA:CP_BA + 16], scalar1=0.5, scalar2=None, op0=ALU.mult), R=[B_c], W=[B_c])
            fw.op("dve", lambda e: e.tensor_scalar(out=dc[:, DC_HBM:DC_HBM + 24], in0=cp[:, CP_BM:CP_BM + 24], scalar1=0.5, scalar2=None, op0=ALU.mult), R=[B_c], W=[B_c])
            t0 = dc[:, DC_T0:DC_T0 + 8]; t1 = dc[:, DC_T1:DC_T1 + 8]; t2 = dc[:, DC_T2:DC_T2 + 8]; t3 = dc[:, DC_T3:DC_T3 + 8]
            fw.op("dve", lambda e: e.tensor_scalar(out=t0, in0=cp[:, CP_LAM:CP_LAM + 8], scalar1=-1.0, scalar2=None, op0=ALU.mult), R=[B_c], W=[B_c])
            fw.op("dve", lambda e: e.tensor_tensor(out=t1, in0=t0, in1=cp[:, CP_LAM:CP_LAM + 8], op=ALU.max), R=[B_c], W=[B_c])
            fw.op("act", lambda e: e.activation(out=t2, in_=t1, func=AF.Exp, scale=-1.0), R=[B_c], W=[B_c])
            fw.op("act", lambda e: e.activation(out=t2, in_=t2, func=AF.Ln, bias=1.0), R=[B_c], W=[B_c])
            fw.op("dve", lambda e: e.tensor_scalar(out=t3, in0=t0, scalar1=0.0, scalar2=None, op0=ALU.max), R=[B_c], W=[B_c])
            fw.op("dve", lambda e: e.tensor_tensor(out=t3, in0=t3, in1=t2, op=ALU.add), R=[B_c], W=[B_c])
            fw.op("dve", lambda e: e.tensor_scalar(out=dc[:, DC_CF:DC_CF + 8], in0=t3, scalar1=-8.0, scalar2=None, op0=ALU.mult), R=[B_c], W=[B_c])
            fw.op("dve", lambda e: e.tensor_scalar(out=dc[:, DC_CH:DC_CH + 8], in0=t3, scalar1=-4.0, scalar2=None, op0=ALU.mult), R=[B_c], W=[B_c])

            for wi, (src, dst) in enumerate((("lru_wa", wab), ("lru_wx", wxb))):
                fw.dma(lambda e, src=src: e.dma_start(out=stage[0][:, :, 0:128], in_=dr[src].rearrange("n i j -> i n j")), d_stage[0], W=[B_stage[0]])
                fw.op("dve", lambda e, dst=dst: e.tensor_copy(out=dst[:], in_=stage[0][:, :, 0:128]), R=[B_stage[0]], W=[B_c])
            for hf in range(2):
                fw.dma(lambda e, hf=hf: e.dma_start(out=stage[1][:].rearrange("p a b -> p (a b)")[:, 0:2560],
                                                    in_=dr["biasr"][:, 4 * hf:4 * hf + 4, :].rearrange("p a b -> p (a b)")), d_stage[1], W=[B_stage[1]])
                fw.op("act", lambda e, hf=hf: e.activation(out=biasR[:, 4 * hf:4 * hf + 4, :].rearrange("p a b -> p (a b)"),
                                                           in_=stage[1][:].rearrange("p a b -> p (a b)")[:, 0:2560], func=AF.Exp), R=[B_stage[1]], W=[B_c])
            fw.op("pool", lambda e: e.memset(biasR[64:128, :, 0:64], 0.0), R=[B_c], W=[B_c])
            fw.op("pool", lambda e: e.memset(biasR[0:64, :, 576:640], 0.0), R=[B_c], W=[B_c])
            fw.op("pool", lambda e: e.memset(dc[:, DC_EPS:DC_EPS + 1], EPS), W=[B_c])
            pat8 = ("dve", "act", "pool", "dve", "act", "pool", "dve", "act")
            pat2 = ("dve", "pool")
            def unit_load(u):
                name, src, c0, rs = units[u]
                k = u % NSB
                fw.dma(lambda e, k=k, src=src, c0=c0: e.dma_start(out=stage[k][:], in_=dr[src].rearrange("(kc p) n -> p kc n", p=128)[:, :, c0:c0 + 512]),
                       d_stage[k], W=[B_stage[k]])
            for u in range(min(NSB - 1, len(units))):
                unit_load(u)
            for u, (name, src, c0, rs) in enumerate(units):
                k = u % NSB
                if u + NSB - 1 < len(units):
                    unit_load(u + NSB - 1)
                if rs in ("ng", "mng"):
                    gc = CP_NG if rs == "ng" else CP_MNG
                    for kc in range(8):
                        eng = pat8[kc]
                        if eng == "act":
                            fw.op("act", lambda e, k=k, kc=kc, gc=gc: e.activation(out=cvt[k][:, kc, :], in_=stage[k][:, kc, :], func=AF.Copy, scale=cp[:, gc + kc:gc + kc + 1]),
                                  R=[B_stage[k], B_c], W=[B_cvt[k]])
                        else:
                            fw.op(eng, lambda e, k=k, kc=kc, gc=gc: e.tensor_scalar(out=cvt[k][:, kc, :], in0=stage[k][:, kc, :], scalar1=cp[:, gc + kc:gc + kc + 1], scalar2=0.0, op0=ALU.mult, op1=ALU.add),
                                  R=[B_stage[k], B_c], W=[B_cvt[k]])
                else:
                    sc = 1.0 if rs is None else float(rs)
                    for half in range(2):
                        eng = pat2[half]
                        sl = slice(4 * half, 4 * half + 4)
                        if eng == "act":
                            fw.op("act", lambda e, k=k, sl=sl, sc=sc: e.activation(out=cvt[k][:, sl, :], in_=stage[k][:, sl, :], func=AF.Copy, scale=sc), R=[B_stage[k]], W=[B_cvt[k]])
                        else:
                            fw.op(eng, lambda e, k=k, sl=sl, sc=sc: e.tensor_scalar(out=cvt[k][:, sl, :], in0=stage[k][:, sl, :], scalar1=sc, scalar2=0.0, op0=ALU.mult, op1=ALU.add), R=[B_stage[k]], W=[B_cvt[k]])
                fw.dma(lambda e, k=k, u=u: e.dma_start(out=scr[u], in_=cvt[k][:].rearrange("p a b -> p (a b)")), d_cvt[k], R=[B_cvt[k]], W=[B_scr[u]])
            fw.barrier()
            fw.emit()
        for b in B_scr:
            b.last_write = None
            b.reads = []

        with contextlib.ExitStack() as st2:
            fw = FW(nc, st0, "b")

            def T(name, shape, dt):
                return st2.enter_context(nc.sbuf_tensor(name, shape, dt))

            def PS(name, shape, dt):
                return st2.enter_context(nc.psum_tensor(name, shape, dt))

            wslot = [T("wslot%d" % i, [128, 8, 512], BF16) for i in range(NSLOT)]
            B_w = [Buf() for _ in range(NSLOT)]
            d_w = [fw.dsem("dw%d" % i) for i in range(NSLOT)]
            hTs = [T("hT%d" % i, [128, 8, 512], BF16) for i in range(2)]; B_hTs = [Buf("hT0"), Buf("hT1")]
            xt = [T("xt%d" % i, [128, 1024], F32) for i in range(2)]; B_xt = [Buf(), Buf()]
            d_xt = [fw.dsem("dxt0"), fw.dsem("dxt1")]
            xs = [T("xs%d" % i, [128, 1024], BF16) for i in range(2)]; B_xs = [Buf(), Buf()]
            st_s = T("st_s", [128, 16], F32)
            B_st = [Buf() for _ in range(4)]
            sg = T("sg", [128, 8, 512], BF16); B_sg = [Buf() for _ in range(8)]
            zT = T("zT", [128, 8, 512], BF16); B_zT = [Buf() for _ in range(8)]
            gt = T("gt", [128, 4, 512], BF16); B_gt = [Buf() for _ in range(4)]
            yacc = T("yacc", [128, 8, 512], F32); B_y = [Buf() for _ in range(8)]
            xrbuf = [T("xrbuf%d" % i, [128, 516], F32) for i in range(2)]; B_xr = [Buf(), Buf()]
            xc = [T("xc%d" % i, [128, 512], F32) for i in range(2)]; B_xc = [Buf(), Buf()]
            xcb = [T("xcb%d" % i, [128, 512], BF16) for i in range(2)]; B_xcb = [Buf(), Buf()]
            tr = [T("tr%d" % i, [128, 512], F32) for i in range(2)]; B_tr = [Buf(), Buf()]
            ti = [T("ti%d" % i, [128, 512], F32) for i in range(2)]; B_ti = [Buf(), Buf()]
            a2 = [T("a2%d" % i, [128, 512], F32) for i in range(2)]; B_a2 = [Buf(), Buf()]
            hb = [T("hb%d" % i, [128, 512], F32) for i in range(2)]; B_hb = [Buf(), Buf()]
            hist = T("hist", [128, 8, 4], F32); B_hist = [Buf() for _ in range(8)]
            hst = T("hst", [128, 8], F32); B_hst = [Buf() for _ in range(8)]
            QT = T("QT", [128, 8, 512], BF16); B_QT = [Buf() for _ in range(8)]
            KT = T("KT", [128, 8, 1024], BF16); B_KT = [Buf() for _ in range(8)]
            Vr = T("Vr", [128, 8, 1024], BF16); B_Vr = [Buf() for _ in range(8)]
            sqj = [T("sqj0", [128, 512], F32)]; B_sqj = [Buf()]
            junk = sqj[0][:].bitcast(BF16)
            NQS = 4
            nst = T("nst", [128, NQS, 12], F32); B_nst = [Buf() for _ in range(NQS)]
            qs = [T("qs%d" % i, [128, 512], BF16) for i in range(NQS)]; B_qs = [Buf() for _ in range(NQS)]
            Pe = [T("Pe%d" % i, [128, 640], BF16) for i in range(2)]; B_Pe = [Buf(), Buf()]
            PT = [T("PT%d" % i, [128, 640], BF16) for i in range(2)]; B_PT = [Buf(), Buf()]
            rD = [T("rD%d" % i, [128, 128], F32) for i in range(2)]; B_rD = [Buf(), Buf()]
            zt = [T("zt%d" % i, [128, 128], F32) for i in range(2)]; B_zt = [Buf(), Buf()]
            PmT = [T("PmT0", [128, 2, 512], BF16)]; B_PmT = [Buf()]
            rDm = T("rDm", [128, 512], F32); B_rDm = Buf()
            ztm = T("ztm", [128, 512], F32); B_ztm = Buf()
            KmT = T("KmT", [128, 8, 256], BF16); B_KmT = Buf()
            Vm = T("Vm", [128, 2, 1024], BF16); B_Vm = Buf()
            tmpm = [T("tmpm0", [128, 512], F32)]; B_tmpm = [Buf()]
            d_out = [fw.dsem("do0"), fw.dsem("do1")]
            NUB = 4
            Ub = [PS("U%d" % i, [128, 512], F32) for i in range(NUB)]; B_U = [Buf() for _ in range(NUB)]
            pT = [PS("pT%d" % i, [128, 1024], BF16) for i in range(2)]; B_pT = [Buf(), Buf()]
            S = PS("S", [128, 1024], F32); B_S = Buf()

            class Stt:
                pass
            stt = Stt()
            stt.u = 0
            stt.p = 0
            stt.wi = 0
            stt.wl = 0
            stt.cnt = {}
            stt.extra = []

            def rr(key, n=2):
                v = stt.cnt.get(key, 0)
                stt.cnt[key] = v + 1
                return v % n

            pq = []

            def push(A, B=None, lag=3):
                A()
                for it in pq:
                    it[0] -= 1
                while pq and pq[0][0] <= 0:
                    pq.pop(0)[1]()
                if B is not None:
                    pq.append([lag, B])

            def flush():
                while pq:
                    pq.pop(0)[1]()

            ts = tile_stream(U)
            stream = []
            for s_ in range(nseq):
                stream += [U["MK0"], U["MK1"], U["MV0"], U["MV1"]]
                for t_ in range(nt):
                    stream += ts

            slot_held = [False] * NSLOT
            slot_of = {}

            def load_next():
                if stt.wl >= len(stream):
                    return False
                for k in range(NSLOT):
                    if not slot_held[k]:
                        j = stt.wl
                        uid = stream[j]
                        fw.dma(lambda e, k=k, uid=uid: e.dma_start(out=wslot[k][:].rearrange("p a b -> p (a b)"), in_=scr[uid]), d_w[k], R=[B_scr[uid]], W=[B_w[k]])
                        slot_held[k] = True
                        slot_of[j] = k
                        stt.wl += 1
                        return True
                return False

            def next_unit(expect):
                i = stt.wi
                assert stream[i] == expect, (i, stream[i], expect)
                while i not in slot_of:
                    assert load_next(), "no free weight slot"
                while load_next():
                    pass
                stt.wi += 1
                return slot_of[i]

            def release(k):
                slot_held[k] = False
                load_next()

            def getU():
                k = stt.u % NUB
                stt.u += 1
                return Ub[k], B_U[k]

            def getpT():
                k = stt.p % 2
                stt.p += 1
                return pT[k], B_pT[k]

            def mm_fm(k, c4, rhs3, R_rhs):
                u, bu = getU()
                for kc in range(8):
                    fw.op("pe", lambda e, u=u, k=k, kc=kc, c4=c4: e.matmul(u[:], lhsT=wslot[k][:, kc, c4 * 128:(c4 + 1) * 128], rhs=rhs3[:, kc, :], start=(kc == 0), stop=(kc == 7)),
                          R=[B_w[k]] + R_rhs, W=[bu])
                return u, bu

            def mm_tm(k, lhs3, s, R_lhs):
                u, bu = getU()
                for kc in range(8):
                    fw.op("pe", lambda e, u=u, k=k, kc=kc, s=s: e.matmul(u[:], lhsT=lhs3[:, kc, s * 128:(s + 1) * 128], rhs=wslot[k][:, kc, :], start=(kc == 0), stop=(kc == 7)),
                          R=[B_w[k]] + R_rhs_fix(R_lhs), W=[bu])
                return u, bu

            def R_rhs_fix(x):
                return list(x)

            def norm_rows_job(src_rows_ap, dest3, dest_cols, B_dest):
                st = {}

                def A():
                    k = rr("xt")
                    st["k"] = k
                    fw.dma(lambda e, k=k: e.dma_start(out=xt[k][:], in_=src_rows_ap), d_xt[k], W=[B_xt[k]])
                    si = rr("st", 4)
                    ssq = st_s[:, si:si + 1]; ms = st_s[:, 4 + si:5 + si]; rstd = st_s[:, 8 + si:9 + si]
                    fw.op("act", lambda e, k=k: e.activation(out=junk, in_=xt[k][:], func=AF.Square, accum_out=ssq), R=[B_xt[k]], W=[B_sqj[0], B_st[si]])
                    fw.op("pool", lambda e: e.tensor_scalar(out=ms, in0=ssq, scalar1=1.0 / DM, scalar2=EPS, op0=ALU.mult, op1=ALU.add), R=[B_st[si]], W=[B_st[si]])
                    fw.op("pool", lambda e: e.tensor_tensor(out=rstd, in0=ms, in1=dc[:, DC_MH:DC_MH + 1], op=ALU.pow), R=[B_st[si]], W=[B_st[si]])
                    fw.op("act", lambda e, k=k: e.activation(out=xs[k][:], in_=xt[k][:], func=AF.Copy, scale=rstd), R=[B_xt[k], B_st[si]], W=[B_xs[k]])

                def Bp():
                    k = st["k"]
                    p, bp = getpT()
                    for kc in range(8):
                        fw.op("pe", lambda e, k=k, kc=kc, p=p: e.transpose(out=p[:, kc * 128:(kc + 1) * 128], in_=xs[k][:, kc * 128:(kc + 1) * 128], identity=ident[:]),
                              R=[B_xs[k]], W=[bp])
                    fw.op("dve", lambda e, p=p: e.tensor_copy(out=dest3[:, :, dest_cols], in_=p[:].rearrange("p (a b) -> p a b", a=8)), R=[bp], W=list(B_dest))
                return A, Bp

            def tm_norm_job(mmfn, nh, dests):
                hd = 512 // nh
                st = {}

                def A():
                    u, bu = mmfn()
                    i = rr("nst", NQS)
                    st["i"] = i
                    ssq = nst[:, i, 0:nh]; sd = nst[:, i, 4:4 + nh]; rs = nst[:, i, 8:8 + nh]
                    j = 0
                    fw.op("act", lambda e: e.activation(out=sqj[j][:], in_=u[:], func=AF.Square), R=[bu], W=[B_sqj[j]])
                    fw.op("dve", lambda e: e.tensor_reduce(out=ssq, in_=sqj[j][:].rearrange("p (a b) -> p a b", a=nh), axis=AX.X, op=ALU.add), R=[B_sqj[j]], W=[B_nst[i]])
                    fw.op("act", lambda e: e.activation(out=sd, in_=ssq, func=AF.Sqrt, scale=1.0 / hd, bias=dc[:, DC_EPS:DC_EPS + 1]), R=[B_nst[i]], W=[B_nst[i]])
                    fw.op("dve", lambda e: e.reciprocal(out=rs, in_=sd), R=[B_nst[i]], W=[B_nst[i]])
                    fw.op("dve", lambda e: e.tensor_tensor(out=qs[i][:].rearrange("p (a b) -> p a b", a=nh), in0=u[:].rearrange("p (a b) -> p a b", a=nh),
                                                           in1=rs.unsqueeze(2).to_broadcast([128, nh, hd]), op=ALU.mult), R=[bu, B_nst[i]], W=[B_qs[i]])

                def Bp():
                    i = st["i"]
                    p, bp = getpT()
                    for c in range(4):
                        fw.op("pe", lambda e, c=c, p=p: e.transpose(out=p[:, c * 128:(c + 1) * 128], in_=qs[i][:, c * 128:(c + 1) * 128], identity=ident[:]), R=[B_qs[i]], W=[bp])
                    for (dap, srcsel, gain, bds) in dests:
                        fw.op("act", lambda e, dap=dap, srcsel=srcsel, gain=gain, p=p: e.activation(out=dap, in_=srcsel(p), func=AF.Copy, scale=gain), R=[bp], W=bds)
                return A, Bp

            def merge_jobs(b, hh, B_zsrc, hT, B_hT, zsrc=None, use_extra=True):
                zsrc = zT if zsrc is None else zsrc
                stq = {}
                jobs = []
                for c4 in range(4):
                    def gate_job(c4=c4):
                        if c4 == 0:
                            stq["k"] = next_unit(U["MG%d_%d" % (b, hh)])
                        k = stq["k"]

                        def A():
                            u, bu = mm_fm(k, c4, hT, [B_hT])
                            col = DC_HBM + b * 8 + hh * 4 + c4
                            fw.op("act", lambda e: e.activation(out=gt[:, c4, :], in_=u[:], func=AF.Tanh, scale=0.5, bias=dc[:, col:col + 1]), R=[bu], W=[B_gt[c4]])
                        push(A)
                        if c4 == 1 and use_extra and stt.extra:
                            stt.extra.pop(0)()
                        if c4 == 3:
                            release(k)
                    jobs.append(gate_job)
                for c4 in range(4):
                    def proj_job(c4=c4):
                        if c4 == 0:
                            stq["k2"] = next_unit(U["P%d_%d" % (b, hh)])
                        k2 = stq["k2"]

                        def A():
                            c = hh * 4 + c4
                            u, bu = mm_fm(k2, c4, zsrc, list(B_zsrc))
                            if b == 0:
                                fw.op("dve", lambda e: e.scalar_tensor_tensor(out=yacc[:, c, :], in0=gt[:, c4, :], scalar=1.0, in1=u[:], op0=ALU.add, op1=ALU.mult),
                                      R=[B_gt[c4], bu], W=[B_y[c]])
                            else:
                                m = 0
                                fw.op("dve", lambda e: e.scalar_tensor_tensor(out=tmpm[m][:], in0=gt[:, c4, :], scalar=1.0, in1=u[:], op0=ALU.add, op1=ALU.mult),
                                      R=[B_gt[c4], bu], W=[B_tmpm[m]])
                                if b == 1:
                                    fw.op("pool", lambda e: e.tensor_tensor(out=yacc[:, c, :], in0=yacc[:, c, :], in1=tmpm[m][:], op=ALU.add), R=[B_y[c], B_tmpm[m]], W=[B_y[c]])
                                else:
                                    fw.op("pool", lambda e: e.tensor_tensor(out=sg[:, c, :], in0=yacc[:, c, :], in1=tmpm[m][:], op=ALU.add), R=[B_y[c], B_tmpm[m]], W=[B_sg[c]])
                        push(A)
                        if c4 == 1 and use_extra and stt.extra:
                            stt.extra.pop(0)()
                        if c4 == 3:
                            release(k2)
                    jobs.append(proj_job)
                return jobs

            def merge_half(b, hh, B_zsrc, hT, B_hT, zsrc=None):
                for job in merge_jobs(b, hh, B_zsrc, hT, B_hT, zsrc=zsrc):
                    job()

            def mem_prepass(seq):
                memT = zT
                for ms_ in range(2):
                    r0 = seq * MEMT + ms_ * 128
                    A, Bp = norm_rows_job(dr["mem"][r0:r0 + 128, :], memT, slice(ms_ * 128, (ms_ + 1) * 128), B_zT)
                    A(); Bp()
                for hh in range(2):
                    k = next_unit(U["MK%d" % hh])
                    for ms_ in range(2):
                        dests = []
                        for par in range(2):
                            dap = KmT[:, 4 * hh + par:4 * hh + par + 3:2, ms_ * 128:(ms_ + 1) * 128]
                            dests.append((dap, (lambda p, par=par: p[:].rearrange("p (a b) -> p a b", a=8)[:, par:par + 3:2, :]), cp[:, CP_MKG + par:CP_MKG + par + 1], [B_KmT]))
                        push(*tm_norm_job((lambda k=k, ms_=ms_: mm_tm(k, memT, ms_, list(B_zT))), 2, dests))
                    release(k)
                for hh in range(2):
                    k = next_unit(U["MV%d" % hh])
                    for ms_ in range(2):
                        def A(k=k, ms_=ms_, hh=hh):
                            u, bu = mm_tm(k, memT, ms_, list(B_zT))
                            fw.op("act", lambda e: e.activation(out=Vm[:, ms_, hh * 512:(hh + 1) * 512], in_=u[:], func=AF.Copy), R=[bu], W=[B_Vm])
                        push(A)
                    release(k)
                flush()
                fw.op("pool", lambda e: e.memset(hist[:], 0.0), W=B_hist)
                fw.op("pool", lambda e: e.memset(hst[:], 0.0), W=B_hst)

            def stage0_jobs(seq, t, hT, B_hT):
                row0 = seq * SEQ + t * TT
                return [norm_rows_job(dr["x"][row0 + s * 128:row0 + (s + 1) * 128, :], hT, slice(s * 128, (s + 1) * 128), [B_hT]) for s in range(4)]

            def tile(seq, t, gi, nxt):
                row0 = seq * SEQ + t * TT
                hT = hTs[gi % 2]; B_hT = B_hTs[gi % 2]

                for hh in range(2):
                    k = next_unit(U["GR%d" % hh])
                    for c4 in range(4):
                        n = hh * 4 + c4
                        u, bu = mm_fm(k, c4, hT, [B_hT])
                        fw.op("act", lambda e, u=u, n=n: e.activation(out=sg[:, n, :], in_=u[:], func=AF.Silu), R=[bu], W=[B_sg[n]])
                    release(k)

                def lru_front(k, c4, n):
                    i = n % 2
                    u, bu = mm_fm(k, c4, hT, [B_hT])
                    fw.op("act", lambda e, u=u, i=i: e.activation(out=xrbuf[i][:, 3:515], in_=u[:], func=AF.Copy), R=[bu], W=[B_xr[i]])
                    fw.op("pool", lambda e, i=i, n=n: e.tensor_copy(out=xrbuf[i][:, 0:3], in_=hist[:, n, 0:3]), R=[B_hist[n]], W=[B_xr[i]])
                    fw.op("pool", lambda e, i=i, n=n: e.tensor_copy(out=hist[:, n, 0:3], in_=xrbuf[i][:, 512:515]), R=[B_xr[i]], W=[B_hist[n]])
                    cw = lambda j: cp[:, CP_CW + n * 4 + j:CP_CW + n * 4 + j + 1]
                    fw.op("dve", lambda e, i=i, n=n: e.tensor_scalar(out=xc[i][:], in0=xrbuf[i][:, 0:512], scalar1=cw(0), scalar2=cp[:, CP_CB + n:CP_CB + n + 1], op0=ALU.mult, op1=ALU.add),
                          R=[B_xr[i]], W=[B_xc[i]])
                    for j in range(1, 4):
                        fw.op("dve", lambda e, i=i, j=j: e.scalar_tensor_tensor(out=xc[i][:], in0=xrbuf[i][:, j:j + 512], scalar=cw(j), in1=xc[i][:], op0=ALU.mult, op1=ALU.add),
                              R=[B_xr[i], B_xc[i]], W=[B_xc[i]])
                    fw.op("pool", lambda e, i=i: e.tensor_scalar(out=xcb[i][:], in0=xc[i][:], scalar1=1.0, scalar2=0.0, op0=ALU.mult, op1=ALU.add), R=[B_xc[i]], W=[B_xcb[i]])

                def lru_back1(n):
                    i = n % 2
                    u1, bu1 = getU()
                    fw.op("pe", lambda e, u1=u1, i=i, n=n: e.matmul(u1[:], lhsT=wab[:, n, :], rhs=xcb[i][:], start=True, stop=True), R=[B_xcb[i]], W=[bu1])
                    u2, bu2 = getU()
                    fw.op("pe", lambda e, u2=u2, i=i, n=n: e.matmul(u2[:], lhsT=wxb[:, n, :], rhs=xcb[i][:], start=True, stop=True), R=[B_xcb[i]], W=[bu2])
                    fw.op("act", lambda e, u1=u1, i=i, n=n: e.activation(out=tr[i][:], in_=u1[:], func=AF.Tanh, scale=0.5, bias=dc[:, DC_HBA + n:DC_HBA + n + 1]), R=[bu1], W=[B_tr[i]])
                    fw.op("act", lambda e, u2=u2, i=i, n=n: e.activation(out=ti[i][:], in_=u2[:], func=AF.Tanh, scale=0.5, bias=dc[:, DC_HBX + n:DC_HBX + n + 1]), R=[bu2], W=[B_ti[i]])
                    fw.op("act", lambda e, i=i, n=n: e.activation(out=tr[i][:], in_=tr[i][:], func=AF.Exp, scale=dc[:, DC_CH + n:DC_CH + n + 1], bias=dc[:, DC_CH + n:DC_CH + n + 1]), R=[B_tr[i]], W=[B_tr[i]])
                    fw.op("pool", lambda e, i=i: e.tensor_tensor(out=a2[i][:], in0=tr[i][:], in1=tr[i][:], op=ALU.mult), R=[B_tr[i]], W=[B_a2[i]])

                def lru_back2(n):
                    i = n % 2
                    fw.op("act", lambda e, i=i: e.activation(out=a2[i][:], in_=a2[i][:], func=AF.Sqrt, scale=-1.0, bias=1.0), R=[B_a2[i]], W=[B_a2[i]])
                    fw.op("dve", lambda e, i=i: e.scalar_tensor_tensor(out=ti[i][:], in0=ti[i][:], scalar=1.0, in1=xc[i][:], op0=ALU.add, op1=ALU.mult), R=[B_ti[i], B_xc[i]], W=[B_ti[i]])
                    fw.op("dve", lambda e, i=i: e.scalar_tensor_tensor(out=ti[i][:], in0=ti[i][:], scalar=0.5, in1=a2[i][:], op0=ALU.mult, op1=ALU.mult), R=[B_ti[i], B_a2[i]], W=[B_ti[i]])
                    fw.op("dve", lambda e, i=i, n=n: e.tensor_tensor_scan(out=hb[i][:], data0=tr[i][:], data1=ti[i][:], initial=hst[:, n:n + 1], op0=ALU.mult, op1=ALU.add),
                          R=[B_tr[i], B_ti[i], B_hst[n]], W=[B_hb[i]])
                    fw.op("pool", lambda e, i=i, n=n: e.tensor_copy(out=hst[:, n:n + 1], in_=hb[i][:, 511:512]), R=[B_hb[i]], W=[B_hst[n]])
                    fw.op("pool", lambda e, i=i, n=n: e.tensor_tensor(out=zT[:, n, :], in0=hb[i][:], in1=sg[:, n, :], op=ALU.mult), R=[B_hb[i], B_sg[n]], W=[B_zT[n]])

                def unit_Q(nm, hh, hooks={}):
                    k = next_unit(U["%s%d" % (nm, hh)])
                    for s in range(4):
                        if nm == "Q":
                            dap = QT[:, 4 * hh:4 * hh + 4, s * 128:(s + 1) * 128]
                            bds = [B_QT[4 * hh + i] for i in range(4)]
                            gain = cp[:, CP_QG:CP_QG + 1]
                        else:
                            slot = (4 * t + s) % 8
                            dap = KT[:, 4 * hh:4 * hh + 4, slot * 128:(slot + 1) * 128]
                            bds = [B_KT[slot]]
                            gain = cp[:, CP_KG:CP_KG + 1]
                        push(*tm_norm_job((lambda k=k, s=s: mm_tm(k, hT, s, [B_hT])), 4,
                                          [(dap, (lambda p: p[:].rearrange("p (a b) -> p a b", a=8)[:, 0:4, :]), gain, bds)]))
                        if s in hooks:
                            hooks[s]()
                    release(k)

                def unit_V(hh, hooks={}):
                    k = next_unit(U["V%d" % hh])
                    for s in range(4):
                        def A(k=k, s=s, hh=hh):
                            slot = (4 * t + s) % 8
                            u, bu = mm_tm(k, hT, s, [B_hT])
                            if s % 2 == 0:
                                fw.op("act", lambda e: e.activation(out=Vr[:, slot, hh * 512:(hh + 1) * 512], in_=u[:], func=AF.Copy), R=[bu], W=[B_Vr[slot]])
                            else:
                                fw.op("dve", lambda e: e.tensor_copy(out=Vr[:, slot, hh * 512:(hh + 1) * 512], in_=u[:]), R=[bu], W=[B_Vr[slot]])
                        push(A)
                        if s in hooks:
                            hooks[s]()
                    release(k)

                def unit_GA(hh, hooks={}):
                    k = next_unit(U["GA%d" % hh])
                    for c4 in range(4):
                        def A(k=k, c4=c4, hh=hh):
                            n = hh * 4 + c4
                            u, bu = mm_fm(k, c4, hT, [B_hT])
                            fw.op("act", lambda e: e.activation(out=sg[:, n, :], in_=u[:], func=AF.Silu), R=[bu], W=[B_sg[n]])
                        push(A)
                        if c4 in hooks:
                            hooks[c4]()
                    release(k)

                others = [lambda hk: unit_Q("Q", 0, hk), lambda hk: unit_Q("Q", 1, hk), lambda hk: unit_Q("K", 0, hk), lambda hk: unit_Q("K", 1, hk),
                          lambda hk: unit_V(0, hk), lambda hk: unit_V(1, hk), lambda hk: unit_GA(0, hk), None]
                kx = None
                for n in range(8):
                    if n % 4 == 0:
                        kx = next_unit(U["XR%d" % (n // 4)])
                    lru_front(kx, n % 4, n)
                    if n % 4 == 3:
                        release(kx)
                    hk = {}
                    if n > 0:
                        hk = {0: (lambda n=n: lru_back1(n - 1)), 1: (lambda n=n: lru_back2(n - 1))}
                    if others[n] is not None:
                        others[n](hk)
                    else:
                        for f in hk.values():
                            f()
                lru_back1(7)
                lru_back2(7)
                unit_GA(1)
                flush()
                jobsA = merge_jobs(0, 0, B_zT, hT, B_hT, use_extra=False) + merge_jobs(0, 1, B_zT, hT, B_hT, use_extra=False)

                scale = 128.0 ** -0.5

                def att_scores(j, h):
                    J = 4 * t + j
                    nd = min(J, 4) + 1
                    for d in range(nd):
                        slot = (J - d) % 8
                        fw.op("pe", lambda e, d=d, slot=slot, h=h, j=j: e.matmul(S[:, d * 128:(d + 1) * 128], lhsT=KT[:, h, slot * 128:(slot + 1) * 128],
                                                                                 rhs=QT[:, h, j * 128:(j + 1) * 128], start=True, stop=True),
                              R=[B_KT[slot], B_QT[h]], W=[B_S])
                    i = rr("Pe")
                    fw.op("act", lambda e, i=i, nd=nd: e.activation(out=Pe[i][:, 0:nd * 128], in_=S[:, 0:nd * 128], func=AF.Exp, scale=scale), R=[B_S], W=[B_Pe[i]])
                    fw.op("dve", lambda e, i=i, nd=nd, h=h: e.tensor_tensor(out=PT[i][:, 0:nd * 128], in0=Pe[i][:, 0:nd * 128], in1=biasR[:, h, 0:nd * 128], op=ALU.mult),
                          R=[B_Pe[i]], W=[B_PT[i]])
                    return i, nd

                def att_pv(j, h, i, nd):
                    J = 4 * t + j
                    od, bod = getU()
                    for d in range(nd):
                        slot = (J - d) % 8
                        fw.op("pe", lambda e, d=d, slot=slot, h=h, od=od, i=i: e.matmul(od[:, 0:128], lhsT=Vr[:, slot, h * 128:(h + 1) * 128],
                                                                                       rhs=PT[i][:, d * 128:(d + 1) * 128], start=(d == 0), stop=(d == nd - 1)),
                              R=[B_Vr[slot], B_PT[i]], W=[bod])
                    for d in range(nd):
                        fw.op("pe", lambda e, d=d, od=od, i=i: e.matmul(od[:, 128:256], lhsT=ones[:], rhs=PT[i][:, d * 128:(d + 1) * 128], start=(d == 0), stop=(d == nd - 1)),
                              R=[B_PT[i]], W=[bod])
                    r = rr("rD")
                    fw.op("act", lambda e, od=od, r=r: e.activation(out=rD[r][:], in_=od[:, 128:256], func=AF.Ln), R=[bod], W=[B_rD[r]])
                    fw.op("act", lambda e, r=r: e.activation(out=rD[r][:], in_=rD[r][:], func=AF.Exp, scale=-1.0), R=[B_rD[r]], W=[B_rD[r]])
                    fw.op("dve", lambda e, od=od, r=r: e.tensor_tensor(out=zt[r][:], in0=od[:, 0:128], in1=rD[r][:], op=ALU.mult), R=[bod, B_rD[r]], W=[B_zt[r]])
                    fw.op("pool", lambda e, r=r, h=h, j=j: e.tensor_tensor(out=sg[:, h, j * 128:(j + 1) * 128], in0=zt[r][:], in1=sg[:, h, j * 128:(j + 1) * 128], op=ALU.mult),
                          R=[B_zt[r], B_sg[h]], W=[B_sg[h]])

                pend = None
                for j in range(4):
                    for h in range(8):
                        if j == 3 and h == 0 and nxt is not None:
                            jobs = stage0_jobs(nxt[0], nxt[1], hTs[(gi + 1) % 2], B_hTs[(gi + 1) % 2])
                            jobs[0][0](); jobs[1][0]()
                            stt.extra = [(lambda jobs=jobs: (jobs[0][1](), jobs[2][0]())), (lambda jobs=jobs: (jobs[1][1](), jobs[3][0]())),
                                         (lambda jobs=jobs: jobs[2][1]()), (lambda jobs=jobs: jobs[3][1]())]
                        i, nd = att_scores(j, h)
                        if pend is not None:
                            att_pv(*pend)
                        pend = (j, h, i, nd)
                        if h % 2 == 1 and jobsA:
                            jobsA.pop(0)()
                att_pv(*pend)
                while jobsA:
                    jobsA.pop(0)()
                flush()
                for hh in range(2):
                    merge_half(1, hh, B_sg, hT, B_hT, zsrc=sg)
                flush()
                assert not stt.extra

                for hh in range(2):
                    k = next_unit(U["QM%d" % hh])
                    for s in range(4):
                        dests = []
                        for par in range(2):
                            dap = QT[:, 4 * hh + par:4 * hh + par + 3:2, s * 128:(s + 1) * 128]
                            bds = [B_QT[4 * hh + par], B_QT[4 * hh + par + 2]]
                            dests.append((dap, (lambda p, par=par: p[:].rearrange("p (a b) -> p a b", a=8)[:, par:par + 3:2, :]), cp[:, CP_MQG + par:CP_MQG + par + 1], bds))
                        push(*tm_norm_job((lambda k=k, s=s: mm_tm(k, hT, s, [B_hT])), 2, dests))
                    release(k)
                for hh in range(2):
                    k = next_unit(U["GM%d" % hh])
                    for c4 in range(4):
                        def A(k=k, c4=c4, hh=hh):
                            n = hh * 4 + c4
                            u, bu = mm_fm(k, c4, hT, [B_hT])
                            fw.op("act", lambda e: e.activation(out=sg[:, n, :], in_=u[:], func=AF.Silu), R=[bu], W=[B_sg[n]])
                        push(A)
                    release(k)
                flush()
                mscale = 256.0 ** -0.5
                for hm in range(4):
                    for mc in range(2):
                        for dch in range(2):
                            c = 2 * hm + dch
                            fw.op("pe", lambda e, mc=mc, dch=dch, c=c: e.matmul(S[:, mc * 512:(mc + 1) * 512], lhsT=KmT[:, c, mc * 128:(mc + 1) * 128], rhs=QT[:, c, :],
                                                                                 start=(dch == 0), stop=(dch == 1)), R=[B_KmT, B_QT[c]], W=[B_S])
                    i = 0
                    fw.op("act", lambda e, i=i: e.activation(out=PmT[i][:].rearrange("p a b -> p (a b)"), in_=S[:], func=AF.Exp, scale=mscale), R=[B_S], W=[B_PmT[i]])
                    ud, bud = getU()
                    for mc in range(2):
                        fw.op("pe", lambda e, mc=mc, ud=ud, i=i: e.matmul(ud[:], lhsT=ones[:], rhs=PmT[i][:, mc, :], start=(mc == 0), stop=(mc == 1)), R=[B_PmT[i]], W=[bud])
                    fw.op("act", lambda e, ud=ud: e.activation(out=rDm[:], in_=ud[:], func=AF.Ln), R=[bud], W=[B_rDm])
                    fw.op("act", lambda e: e.activation(out=rDm[:], in_=rDm[:], func=AF.Exp, scale=-1.0), R=[B_rDm], W=[B_rDm])
                    for dch in range(2):
                        c = 2 * hm + dch
                        uo, buo = getU()
                        for mc in range(2):
                            fw.op("pe", lambda e, mc=mc, uo=uo, i=i, c=c: e.matmul(uo[:], lhsT=Vm[:, mc, c * 128:(c + 1) * 128], rhs=PmT[i][:, mc, :], start=(mc == 0), stop=(mc == 1)),
                                  R=[B_Vm, B_PmT[i]], W=[buo])
                        fw.op("dve", lambda e, uo=uo: e.tensor_tensor(out=ztm[:], in0=uo[:], in1=rDm[:], op=ALU.mult), R=[buo, B_rDm], W=[B_ztm])
                        fw.op("pool", lambda e, c=c: e.tensor_tensor(out=zT[:, c, :], in0=ztm[:], in1=sg[:, c, :], op=ALU.mult), R=[B_ztm, B_sg[c]], W=[B_zT[c]])
                xk = {}

                def x_reload(s):
                    k = rr("xt")
                    xk[s] = k
                    rows = slice(row0 + s * 128, row0 + (s + 1) * 128)
                    fw.dma(lambda e, k=k, rows=rows: e.dma_start(out=xt[k][:], in_=dr["x"][rows, :]), d_xt[k], W=[B_xt[k]])
                x_reload(0); x_reload(1)
                for hh in range(2):
                    merge_half(2, hh, B_zT, hT, B_hT)
                flush()

                k0 = next_unit(U["WO0"])
                k1 = next_unit(U["WO1"])
                for s in range(4):
                    if s >= 2:
                        x_reload(s)
                    k = xk[s]
                    rows = slice(row0 + s * 128, row0 + (s + 1) * 128)
                    for hh, kw in ((0, k0), (1, k1)):
                        u, bu = mm_tm(kw, sg, s, B_sg)
                        fw.op("dve", lambda e, u=u, k=k, hh=hh: e.tensor_tensor(out=xt[k][:, hh * 512:(hh + 1) * 512], in0=u[:], in1=xt[k][:, hh * 512:(hh + 1) * 512], op=ALU.add),
                              R=[bu, B_xt[k]], W=[B_xt[k]])
                    fw.dma(lambda e, k=k, rows=rows: e.dma_start(out=out[rows, :], in_=xt[k][:]), d_out[k], R=[B_xt[k]])
                release(k0)
                release(k1)

            tiles = [(sq_, t_) for sq_ in range(nseq) for t_ in range(nt)]
            for A, Bp in stage0_jobs(0, 0, hTs[0], B_hTs[0]):
                A(); Bp()
            for gi, (seq, t) in enumerate(tiles):
                if t == 0:
                    mem_prepass(seq)
                nxt = tiles[gi + 1] if gi + 1 < len(tiles) else None
                tile(seq, t, gi, nxt)
            fw.wait_all_dma("sp")
            fw.barrier()
            fw.emit()
    return nc


_NC_CACHE = {}


def _host_consts(inp):
    f = lambda a: np.asarray(a, dtype=np.float32)
    cpk = np.zeros((128, NCP), np.float32)
    cpk[:, CP_NG:CP_NG + 8] = f(inp["norm_g"])[0].reshape(8, 128).T
    cpk[:, CP_MNG:CP_MNG + 8] = f(inp["mem_norm_g"])[0].reshape(8, 128).T
    cw = f(inp["conv_w"])[0]
    cpk[:, CP_CW:CP_CW + 32] = cw.reshape(4, 8, 128).transpose(2, 1, 0).reshape(128, 32)
    cpk[:, CP_CB:CP_CB + 8] = f(inp["conv_b"])[0].reshape(8, 128).T
    cpk[:, CP_BA:CP_BA + 8] = f(inp["lru_ba"])[0].reshape(8, 128).T
    cpk[:, CP_BX:CP_BX + 8] = f(inp["lru_bx"])[0].reshape(8, 128).T
    cpk[:, CP_LAM:CP_LAM + 8] = f(inp["lru_lambda"])[0].reshape(8, 128).T
    cpk[:, CP_BM:CP_BM + 24] = f(inp["b_merge"])[0].reshape(24, 128).T
    cpk[:, CP_QG] = f(inp["q_norm_g"])[0]
    cpk[:, CP_KG] = f(inp["k_norm_g"])[0]
    cpk[:, CP_MQG:CP_MQG + 2] = f(inp["mem_q_norm_g"])[0].reshape(2, 128).T
    cpk[:, CP_MKG:CP_MKG + 2] = f(inp["mem_k_norm_g"])[0].reshape(2, 128).T
    rb = f(inp["rel_bias"])[0]
    kl = np.arange(128)[:, None]
    ii = np.arange(640)[None, :]
    idx = np.clip(ii - kl, -256, 256) + 256
    biasr = np.ascontiguousarray(rb[:, idx].transpose(1, 0, 2))
    return cpk, biasr


def kernel(**inp):
    if "nc" not in _NC_CACHE:
        _NC_CACHE["nc"] = build_program()
    nc = _NC_CACHE["nc"]
    f = lambda a: np.ascontiguousarray(np.asarray(a, dtype=np.float32))
    x = f(inp["x"])
    mem = f(inp["mem"])
    cpk, biasr = _host_consts(inp)
    shared = {
        "w_in": f(inp["w_in"])[0], "w_mem_kv": f(inp["w_mem_kv"])[0],
        "w_proj_rnn": f(inp["w_proj_rnn"])[0], "w_proj_att": f(inp["w_proj_att"])[0],
        "w_proj_mem": f(inp["w_proj_mem"])[0], "w_out": f(inp["w_out"])[0],
        "lru_wa": f(inp["lru_wa"])[0], "lru_wx": f(inp["lru_wx"])[0],
        "cpack": cpk, "biasr": biasr,
    }
    in_maps = []
    for c in range(N_CORES):
        m = dict(shared)
        m["x"] = x[NSEQ * c:NSEQ * (c + 1)].reshape(NSEQ * SEQ, DM)
        m["mem"] = mem[NSEQ * c:NSEQ * (c + 1)].reshape(NSEQ * MEMT, DM)
        in_maps.append(m)
    res = run_bass_kernel_spmd(nc, in_maps, core_ids=list(range(N_CORES)))
    outs = [np.asarray(r["out"], dtype=np.float32).reshape(NSEQ, SEQ, DM) for r in res.results]
    return np.concatenate(outs, axis=0)
```

```python
import contextlib
import numpy as np
import concourse.bass as bass
import concourse.mybir as mybir
from concourse.bass_utils import run_bass_kernel_spmd

F32 = mybir.dt.float32
BF16 = mybir.dt.bfloat16
AF = mybir.ActivationFunctionType
ALU = mybir.AluOpType
AX = mybir.AxisListType

N_CORES = 8
SEQ = 4096
DM = 1024
TT = 512
NT = SEQ // TT
NSEQ = 2
MEMT = 256
EPS = 1e-6
NSLOT = 4

C_XR, C_GR, C_Q, C_K, C_V, C_GA, C_QM, C_GM, C_MG = 0, 1024, 2048, 3072, 4096, 5120, 6144, 7168, 8192


class Buf:
    __slots__ = ("name", "last_write", "reads")

    def __init__(self, name=""):
        self.name = name
        self.last_write = None
        self.reads = []


class DSem:
    def __init__(self, sem):
        self.sem = sem
        self.count = 0


class Instr:
    __slots__ = ("fn", "waits", "signal", "dsem", "val")

    def __init__(self, fn, waits, dsem=None):
        self.fn = fn
        self.waits = waits
        self.signal = False
        self.dsem = dsem
        self.val = 0


class FW:
    ENGS = ("pe", "act", "dve", "pool", "sp")

    def __init__(self, nc, stack, tag=""):
        self.nc = nc
        self.stack = stack
        self.tag = tag
        self.instrs = {e: [] for e in self.ENGS}
        self.sems = {e: stack.enter_context(nc.semaphore("s_" + tag + e)) for e in self.ENGS}
        self.seen = {e: {} for e in self.ENGS}
        self.seen_d = {e: {} for e in self.ENGS}
        self.dsems = []

    def dsem(self, name):
        s = self.stack.enter_context(self.nc.semaphore(self.tag + name))
        d = DSem(s)
        self.dsems.append(d)
        return d

    def _deps(self, eng, reads, writes):
        deps = []
        for b in reads:
            if b.last_write is not None:
                deps.append((b.last_write, "raw"))
        for b in writes:
            if b.last_write is not None:
                deps.append((b.last_write, "waw"))
            for r in b.reads:
                deps.append((r, "war"))
        best = {}
        out = []
        for d, kind in deps:
            if d[0] == "c":
                _, e2, idx = d
                if e2 == eng and (kind != "raw" or eng == "pe"):
                    continue
                if self.seen[eng].get(e2, -1) >= idx:
                    continue
                best[e2] = max(best.get(e2, -1), idx)
            else:
                _, ds, cnt = d
                if self.seen_d[eng].get(id(ds), 0) >= cnt:
                    continue
                self.seen_d[eng][id(ds)] = cnt
                out.append(("d", ds, cnt))
        for e2, idx in best.items():
            self.seen[eng][e2] = idx
            out.append(("c", e2, idx))
        return out

    def op(self, eng, fn, R=(), W=()):
        waits = self._deps(eng, R, W)
        idx = len(self.instrs[eng])
        self.instrs[eng].append(Instr(fn, waits))
        tag = ("c", eng, idx)
        for b in R:
            b.reads.append(tag)
        for b in W:
            b.last_write = tag
            b.reads = []
        return tag

    def dma(self, fn, dsem, R=(), W=(), eng="sp"):
        waits = self._deps(eng, R, W)
        dsem.count += 1
        ins = Instr(fn, waits, dsem=dsem)
        ins.val = dsem.count
        self.instrs[eng].append(ins)
        tag = ("d", dsem, dsem.count)
        for b in R:
            b.reads.append(tag)
        for b in W:
            b.last_write = tag
            b.reads = []
        return tag

    def barrier(self):
        last = {}
        for e2 in self.ENGS:
            idx = len(self.instrs[e2]) - 1
            while idx >= 0 and (self.instrs[e2][idx].dsem is not None or self.instrs[e2][idx].fn is None):
                idx -= 1
            last[e2] = idx
        for e in self.ENGS:
            waits = []
            for e2 in self.ENGS:
                idx = last[e2]
                if idx >= 0 and e2 != e and self.seen[e].get(e2, -1) < idx:
                    self.seen[e][e2] = idx
                    waits.append(("c", e2, idx))
            for ds in self.dsems:
                if ds.count > 0 and self.seen_d[e].get(id(ds), 0) < ds.count:
                    self.seen_d[e][id(ds)] = ds.count
                    waits.append(("d", ds, ds.count))
            if waits:
                self.instrs[e].append(Instr(None, waits))

    def wait_all_dma(self, eng="sp"):
        waits = []
        for ds in self.dsems:
            if ds.count > 0 and self.seen_d[eng].get(id(ds), 0) < ds.count:
                self.seen_d[eng][id(ds)] = ds.count
                waits.append(("d", ds, ds.count))
        if waits:
            self.instrs[eng].append(Instr(None, waits))

    def emit(self):
        nc = self.nc
        for e in self.ENGS:
            for ins in self.instrs[e]:
                for w in ins.waits:
                    if w[0] == "c":
                        self.instrs[w[1]][w[2]].signal = True
        for e in self.ENGS:
            c = 0
            for ins in self.instrs[e]:
                if ins.dsem is None and ins.signal:
                    c += 1
                    ins.val = c

        def replay(ename, h):
            for ins in self.instrs[ename]:
                for w in ins.waits:
                    if w[0] == "c":
                        h.wait_ge(self.sems[w[1]], self.instrs[w[1]][w[2]].val)
                    else:
                        h.wait_ge(w[1].sem, 16 * w[2])
                if ins.fn is None:
                    continue
                bi = ins.fn(h)
                if ins.dsem is not None:
                    bi.then_inc(ins.dsem.sem, 16)
                elif ins.signal:
                    bi.then_inc(self.sems[ename], 1)

        with nc.Block() as block:
            @block.tensor
            def _(h):
                replay("pe", h)

            @block.scalar
            def _(h):
                replay("act", h)

            @block.vector
            def _(h):
                replay("dve", h)

            @block.gpsimd
            def _(h):
                replay("pool", h)

            @block.sync
            def _(h):
                replay("sp", h)


def unit_table():
    units = []

    def add(name, src, c0, rs):
        units.append((name, src, c0, rs))
        return len(units) - 1

    U = {}
    for hh in range(2):
        U["MK%d" % hh] = add("MK%d" % hh, "w_mem_kv", hh * 512, "mng")
    for hh in range(2):
        U["MV%d" % hh] = add("MV%d" % hh, "w_mem_kv", 1024 + hh * 512, "mng")
    for nm, c in (("GR", C_GR), ("XR", C_XR), ("Q", C_Q), ("K", C_K), ("V", C_V), ("GA", C_GA), ("QM", C_QM), ("GM", C_GM)):
        for hh in range(2):
            U["%s%d" % (nm, hh)] = add("%s%d" % (nm, hh), "w_in", c + hh * 512, "ng")
    for b in range(3):
        for hh in range(2):
            U["MG%d_%d" % (b, hh)] = add("MG%d_%d" % (b, hh), "w_in", C_MG + b * 1024 + hh * 512, "ng")
    for b, src in enumerate(("w_proj_rnn", "w_proj_att", "w_proj_mem")):
        for hh in range(2):
            U["P%d_%d" % (b, hh)] = add("P%d_%d" % (b, hh), src, hh * 512, None)
    for hh in range(2):
        U["WO%d" % hh] = add("WO%d" % hh, "w_out", hh * 512, 0.5)
    return units, U


def tile_stream(U):
    s = ["GR0", "GR1", "XR0", "Q0", "Q1", "K0", "K1", "XR1", "V0", "V1", "GA0", "GA1", "MG0_0", "P0_0", "MG0_1", "P0_1",
         "MG1_0", "P1_0", "MG1_1", "P1_1",
         "QM0", "QM1", "GM0", "GM1", "MG2_0", "P2_0", "MG2_1", "P2_1", "WO0", "WO1"]
    return [U[k] for k in s]


CP_NG, CP_MNG, CP_CW, CP_CB, CP_BA, CP_BX, CP_LAM, CP_BM, CP_QG, CP_KG, CP_MQG, CP_MKG, NCP = 0, 8, 16, 48, 56, 64, 72, 80, 104, 105, 106, 108, 112
DC_HBA, DC_HBX, DC_HBM, DC_CF, DC_CH, DC_MH, DC_T0, DC_T1, DC_T2, DC_T3, DC_EPS, NDC = 0, 8, 16, 40, 48, 56, 57, 65, 73, 81, 89, 96


def build_program(nseq=NSEQ, nt=NT, level=99):
    nc = bass.Bass("TRN2", target_bir_lowering=False)
    units, U = unit_table()
    NU = len(units)
    dr = {}
    dr["x"] = nc.dram_tensor("x", [NSEQ * SEQ, DM], F32, kind="ExternalInput").ap()
    dr["mem"] = nc.dram_tensor("mem", [NSEQ * MEMT, DM], F32, kind="ExternalInput").ap()
    dr["w_in"] = nc.dram_tensor("w_in", [DM, 11264], F32, kind="ExternalInput").ap()
    dr["w_mem_kv"] = nc.dram_tensor("w_mem_kv", [DM, 2048], F32, kind="ExternalInput").ap()
    for k in ("w_proj_rnn", "w_proj_att", "w_proj_mem", "w_out"):
        dr[k] = nc.dram_tensor(k, [DM, DM], F32, kind="ExternalInput").ap()
    dr["lru_wa"] = nc.dram_tensor("lru_wa", [8, 128, 128], F32, kind="ExternalInput").ap()
    dr["lru_wx"] = nc.dram_tensor("lru_wx", [8, 128, 128], F32, kind="ExternalInput").ap()
    dr["cpack"] = nc.dram_tensor("cpack", [128, NCP], F32, kind="ExternalInput").ap()
    dr["biasr"] = nc.dram_tensor("biasr", [128, 8, 640], F32, kind="ExternalInput").ap()
    out = nc.dram_tensor("out", [NSEQ * SEQ, DM], F32, kind="ExternalOutput").ap()
    scr = nc.dram_tensor("scr", [NU, 128, 4096], BF16, kind="Internal").ap()

    with contextlib.ExitStack() as st0:
        def T0(name, shape, dt):
            return st0.enter_context(nc.sbuf_tensor(name, shape, dt))

        cp = T0("cp", [128, NCP], F32)
        dc = T0("dc", [128, NDC], F32)
        ident = T0("ident", [128, 128], BF16)
        ones = T0("ones", [128, 128], BF16)
        wab = T0("wab", [128, 8, 128], BF16)
        wxb = T0("wxb", [128, 8, 128], BF16)
        biasR = T0("biasR", [128, 8, 640], BF16)
        B_scr = [Buf("scr%d" % i) for i in range(NU)]

        with contextlib.ExitStack() as st1:
            fw = FW(nc, st0, "a")

            def T1(name, shape, dt):
                return st1.enter_context(nc.sbuf_tensor(name, shape, dt))

            NSB = 4
            stage = [T1("stage%d" % i, [128, 8, 512], F32) for i in range(NSB)]
            cvt = [T1("cvt%d" % i, [128, 8, 512], BF16) for i in range(NSB)]
            identf = T1("identf", [128, 128], F32)
            B_stage = [Buf() for _ in range(NSB)]
            B_cvt = [Buf() for _ in range(NSB)]
            B_c = Buf("consts")
            d_stage = [fw.dsem("dst%d" % i) for i in range(NSB)]
            d_cvt = [fw.dsem("dcv%d" % i) for i in range(NSB)]
            d_c = fw.dsem("dc")

            fw.dma(lambda e: e.dma_start(out=cp[:], in_=dr["cpack"]), d_c, W=[B_c])
            fw.op("pool", lambda e: e.memset(identf[:], 0.0), W=[B_c])
            fw.op("pool", lambda e: e.affine_select(out=identf[:], in_=identf[:], compare_op=ALU.not_equal, fill=1.0,
                                                    base=0, pattern=[[-1, 128]], channel_multiplier=1), R=[B_c], W=[B_c])
            fw.op("pool", lambda e: e.tensor_copy(out=ident[:], in_=identf[:]), R=[B_c], W=[B_c])
            fw.op("pool", lambda e: e.memset(ones[:], 1.0), W=[B_c])
            fw.op("pool", lambda e: e.memset(dc[:, DC_MH:DC_MH + 1], -0.5), W=[B_c])
            fw.op("dve", lambda e: e.tensor_scalar(out=dc[:, DC_HBA:DC_HBA + 16], in0=cp[:, CP_BA:CP_BA + 16], scalar1=0.5, scalar2=None, op0=ALU.mult), R=[B_c], W=[B_c])
            fw.op("dve", lambda e: e.tensor_scalar(out=dc[:, DC_HBM:DC_HBM + 24], in0=cp[:, CP_BM:CP_BM + 24], scalar1=0.5, scalar2=None, op0=ALU.mult), R=[B_c], W=[B_c])
            t0 = dc[:, DC_T0:DC_T0 + 8]; t1 = dc[:, DC_T1:DC_T1 + 8]; t2 = dc[:, DC_T2:DC_T2 + 8]; t3 = dc[:, DC_T3:DC_T3 + 8]
            fw.op("dve", lambda e: e.tensor_scalar(out=t0, in0=cp[:, CP_LAM:CP_LAM + 8], scalar1=-1.0, scalar2=None, op0=ALU.mult), R=[B_c], W=[B_c])
            fw.op("dve", lambda e: e.tensor_tensor(out=t1, in0=t0, in1=cp[:, CP_LAM:CP_LAM + 8], op=ALU.max), R=[B_c], W=[B_c])
            fw.op("act", lambda e: e.activation(out=t2, in_=t1, func=AF.Exp, scale=-1.0), R=[B_c], W=[B_c])
            fw.op("act", lambda e: e.activation(out=t2, in_=t2, func=AF.Ln, bias=1.0), R=[B_c], W=[B_c])
            fw.op("dve", lambda e: e.tensor_scalar(out=t3, in0=t0, scalar1=0.0, scalar2=None, op0=ALU.max), R=[B_c], W=[B_c])
            fw.op("dve", lambda e: e.tensor_tensor(out=t3, in0=t3, in1=t2, op=ALU.add), R=[B_c], W=[B_c])
            fw.op("dve", lambda e: e.tensor_scalar(out=dc[:, DC_CF:DC_CF + 8], in0=t3, scalar1=-8.0, scalar2=None, op0=ALU.mult), R=[B_c], W=[B_c])
            fw.op("dve", lambda e: e.tensor_scalar(out=dc[:, DC_CH:DC_CH + 8], in0=t3, scalar1=-4.0, scalar2=None, op0=ALU.mult), R=[B_c], W=[B_c])

            for wi, (src, dst) in enumerate((("lru_wa", wab), ("lru_wx", wxb))):
                fw.dma(lambda e, src=src: e.dma_start(out=stage[0][:, :, 0:128], in_=dr[src].rearrange("n i j -> i n j")), d_stage[0], W=[B_stage[0]])
                fw.op("dve", lambda e, dst=dst: e.tensor_copy(out=dst[:], in_=stage[0][:, :, 0:128]), R=[B_stage[0]], W=[B_c])
            for hf in range(2):
                fw.dma(lambda e, hf=hf: e.dma_start(out=stage[1][:].rearrange("p a b -> p (a b)")[:, 0:2560],
                                                    in_=dr["biasr"][:, 4 * hf:4 * hf + 4, :].rearrange("p a b -> p (a b)")), d_stage[1], W=[B_stage[1]])
                fw.op("act", lambda e, hf=hf: e.activation(out=biasR[:, 4 * hf:4 * hf + 4, :].rearrange("p a b -> p (a b)"),
                                                           in_=stage[1][:].rearrange("p a b -> p (a b)")[:, 0:2560], func=AF.Exp), R=[B_stage[1]], W=[B_c])
            fw.op("pool", lambda e: e.memset(biasR[64:128, :, 0:64], 0.0), R=[B_c], W=[B_c])
            fw.op("pool", lambda e: e.memset(biasR[0:64, :, 576:640], 0.0), R=[B_c], W=[B_c])
            fw.op("pool", lambda e: e.memset(dc[:, DC_EPS:DC_EPS + 1], EPS), W=[B_c])
            pat8 = ("dve", "act", "pool", "dve", "act", "pool", "dve", "act")
            pat2 = ("dve", "pool")
            def unit_load(u):
                name, src, c0, rs = units[u]
                k = u % NSB
                fw.dma(lambda e, k=k, src=src, c0=c0: e.dma_start(out=stage[k][:], in_=dr[src].rearrange("(kc p) n -> p kc n", p=128)[:, :, c0:c0 + 512]),
                       d_stage[k], W=[B_stage[k]], eng=("sp" if u % 2 == 0 else "act"))
            for u in range(min(NSB - 1, len(units))):
                unit_load(u)
            for u, (name, src, c0, rs) in enumerate(units):
                k = u % NSB
                if u + NSB - 1 < len(units):
                    unit_load(u + NSB - 1)
                if rs in ("ng", "mng"):
                    gc = CP_NG if rs == "ng" else CP_MNG
                    for kc in range(8):
                        eng = pat8[kc]
                        if eng == "act":
                            fw.op("act", lambda e, k=k, kc=kc, gc=gc: e.activation(out=cvt[k][:, kc, :], in_=stage[k][:, kc, :], func=AF.Copy, scale=cp[:, gc + kc:gc + kc + 1]),
                                  R=[B_stage[k], B_c], W=[B_cvt[k]])
                        else:
                            fw.op(eng, lambda e, k=k, kc=kc, gc=gc: e.tensor_scalar(out=cvt[k][:, kc, :], in0=stage[k][:, kc, :], scalar1=cp[:, gc + kc:gc + kc + 1], scalar2=0.0, op0=ALU.mult, op1=ALU.add),
                                  R=[B_stage[k], B_c], W=[B_cvt[k]])
                else:
                    sc = 1.0 if rs is None else float(rs)
                    for half in range(2):
                        eng = pat2[half]
                        sl = slice(4 * half, 4 * half + 4)
                        if eng == "act":
                            fw.op("act", lambda e, k=k, sl=sl, sc=sc: e.activation(out=cvt[k][:, sl, :], in_=stage[k][:, sl, :], func=AF.Copy, scale=sc), R=[B_stage[k]], W=[B_cvt[k]])
                        else:
                            fw.op(eng, lambda e, k=k, sl=sl, sc=sc: e.tensor_scalar(out=cvt[k][:, sl, :], in0=stage[k][:, sl, :], scalar1=sc, scalar2=0.0, op0=ALU.mult, op1=ALU.add), R=[B_stage[k]], W=[B_cvt[k]])
                fw.dma(lambda e, k=k, u=u: e.dma_start(out=scr[u], in_=cvt[k][:].rearrange("p a b -> p (a b)")), d_cvt[k], R=[B_cvt[k]], W=[B_scr[u]])
            fw.barrier()
            fw.emit()
        for b in B_scr:
            b.last_write = None
            b.reads = []

        with contextlib.ExitStack() as st2:
            fw = FW(nc, st0, "b")

            def T(name, shape, dt):
                return st2.enter_context(nc.sbuf_tensor(name, shape, dt))

            def PS(name, shape, dt):
                return st2.enter_context(nc.psum_tensor(name, shape, dt))

            wslot = [T("wslot%d" % i, [128, 8, 512], BF16) for i in range(NSLOT)]
            B_w = [Buf() for _ in range(NSLOT)]
            d_w = [fw.dsem("dw%d" % i) for i in range(NSLOT)]
            hTs = [T("hT%d" % i, [128, 8, 512], BF16) for i in range(2)]; B_hTs = [Buf("hT0"), Buf("hT1")]
            xt = [T("xt%d" % i, [128, 1024], F32) for i in range(2)]; B_xt = [Buf(), Buf()]
            d_xt = [fw.dsem("dxt0"), fw.dsem("dxt1")]
            xs = [T("xs%d" % i, [128, 1024], BF16) for i in range(2)]; B_xs = [Buf(), Buf()]
            st_s = T("st_s", [128, 16], F32)
            B_st = [Buf() for _ in range(4)]
            sg = T("sg", [128, 8, 512], BF16); B_sg = [Buf() for _ in range(8)]
            zT = T("zT", [128, 8, 512], BF16); B_zT = [Buf() for _ in range(8)]
            gt = T("gt", [128, 4, 512], BF16); B_gt = [Buf() for _ in range(4)]
            yacc = T("yacc", [128, 8, 512], F32); B_y = [Buf() for _ in range(8)]
            xrbuf = [T("xrbuf%d" % i, [128, 516], F32) for i in range(2)]; B_xr = [Buf(), Buf()]
            xc = [T("xc%d" % i, [128, 512], F32) for i in range(2)]; B_xc = [Buf(), Buf()]
            xcb = [T("xcb%d" % i, [128, 512], BF16) for i in range(2)]; B_xcb = [Buf(), Buf()]
            tr = [T("tr%d" % i, [128, 512], F32) for i in range(2)]; B_tr = [Buf(), Buf()]
            ti = [T("ti%d" % i, [128, 512], F32) for i in range(2)]; B_ti = [Buf(), Buf()]
            a2 = [T("a2%d" % i, [128, 512], F32) for i in range(2)]; B_a2 = [Buf(), Buf()]
            hb = [T("hb%d" % i, [128, 512], F32) for i in range(2)]; B_hb = [Buf(), Buf()]
            hist = T("hist", [128, 8, 4], F32); B_hist = [Buf() for _ in range(8)]
            hst = T("hst", [128, 8], F32); B_hst = [Buf() for _ in range(8)]
            QT = T("QT", [128, 8, 512], BF16); B_QT = [Buf() for _ in range(8)]
            KT = T("KT", [128, 8, 1024], BF16); B_KT = [Buf() for _ in range(8)]
            Vr = T("Vr", [128, 8, 1024], BF16); B_Vr = [Buf() for _ in range(8)]
            sqj = [T("sqj0", [128, 512], F32)]; B_sqj = [Buf()]
            junk = sqj[0][:].bitcast(BF16)
            NQS = 4
            nst = T("nst", [128, NQS, 12], F32); B_nst = [Buf() for _ in range(NQS)]
            qs = [T("qs%d" % i, [128, 512], BF16) for i in range(NQS)]; B_qs = [Buf() for _ in range(NQS)]
            Pe = [T("Pe%d" % i, [128, 640], BF16) for i in range(2)]; B_Pe = [Buf(), Buf()]
            PT = [T("PT%d" % i, [128, 640], BF16) for i in range(2)]; B_PT = [Buf(), Buf()]
            rD = [T("rD%d" % i, [128, 128], F32) for i in range(2)]; B_rD = [Buf(), Buf()]
            zt = [T("zt%d" % i, [128, 128], F32) for i in range(2)]; B_zt = [Buf(), Buf()]
            PmT = [T("PmT0", [128, 2, 512], BF16)]; B_PmT = [Buf()]
            rDm = T("rDm", [128, 512], F32); B_rDm = Buf()
            ztm = T("ztm", [128, 512], F32); B_ztm = Buf()
            KmT = T("KmT", [128, 8, 256], BF16); B_KmT = Buf()
            Vm = T("Vm", [128, 2, 1024], BF16); B_Vm = Buf()
            tmpm = [T("tmpm0", [128, 512], F32)]; B_tmpm = [Buf()]
            d_out = [fw.dsem("do0"), fw.dsem("do1")]
            NUB = 4
            Ub = [PS("U%d" % i, [128, 512], F32) for i in range(NUB)]; B_U = [Buf() for _ in range(NUB)]
            pT = [PS("pT%d" % i, [128, 1024], BF16) for i in range(2)]; B_pT = [Buf(), Buf()]
            S = PS("S", [128, 1024], F32); B_S = Buf()

            class Stt:
                pass
            stt = Stt()
            stt.u = 0
            stt.p = 0
            stt.wi = 0
            stt.wl = 0
            stt.cnt = {}
            stt.extra = []

            def rr(key, n=2):
                v = stt.cnt.get(key, 0)
                stt.cnt[key] = v + 1
                return v % n

            pq = []

            def push(A, B=None, lag=3):
                A()
                for it in pq:
                    it[0] -= 1
                while pq and pq[0][0] <= 0:
                    pq.pop(0)[1]()
                if B is not None:
                    pq.append([lag, B])

            def flush():
                while pq:
                    pq.pop(0)[1]()

            ts = tile_stream(U)
            stream = []
            for s_ in range(nseq):
                stream += [U["MK0"], U["MK1"], U["MV0"], U["MV1"]]
                for t_ in range(nt):
                    stream += ts

            slot_held = [False] * NSLOT
            slot_of = {}

            def load_next():
                if stt.wl >= len(stream):
                    return False
                for k in range(NSLOT):
                    if not slot_held[k]:
                        j = stt.wl
                        uid = stream[j]
                        fw.dma(lambda e, k=k, uid=uid: e.dma_start(out=wslot[k][:].rearrange("p a b -> p (a b)"), in_=scr[uid]), d_w[k], R=[B_scr[uid]], W=[B_w[k]])
                        slot_held[k] = True
                        slot_of[j] = k
                        stt.wl += 1
                        return True
                return False

            def next_unit(expect):
                i = stt.wi
                assert stream[i] == expect, (i, stream[i], expect)
                while i not in slot_of:
                    assert load_next(), "no free weight slot"
                while load_next():
                    pass
                stt.wi += 1
                return slot_of[i]

            def release(k):
                slot_held[k] = False
                load_next()

            def getU():
                k = stt.u % NUB
                stt.u += 1
                return Ub[k], B_U[k]

            def getpT():
                k = stt.p % 2
                stt.p += 1
                return pT[k], B_pT[k]

            def mm_fm(k, c4, rhs3, R_rhs):
                u, bu = getU()
                for kc in range(8):
                    fw.op("pe", lambda e, u=u, k=k, kc=kc, c4=c4: e.matmul(u[:], lhsT=wslot[k][:, kc, c4 * 128:(c4 + 1) * 128], rhs=rhs3[:, kc, :], start=(kc == 0), stop=(kc == 7)),
                          R=[B_w[k]] + R_rhs, W=[bu])
                return u, bu

            def mm_tm(k, lhs3, s, R_lhs):
                u, bu = getU()
                for kc in range(8):
                    fw.op("pe", lambda e, u=u, k=k, kc=kc, s=s: e.matmul(u[:], lhsT=lhs3[:, kc, s * 128:(s + 1) * 128], rhs=wslot[k][:, kc, :], start=(kc == 0), stop=(kc == 7)),
                          R=[B_w[k]] + R_rhs_fix(R_lhs), W=[bu])
                return u, bu

            def R_rhs_fix(x):
                return list(x)

            def norm_rows_job(src_rows_ap, dest3, dest_cols, B_dest):
                st = {}

                def A():
                    k = rr("xt")
                    st["k"] = k
                    fw.dma(lambda e, k=k: e.dma_start(out=xt[k][:], in_=src_rows_ap), d_xt[k], W=[B_xt[k]])
                    si = rr("st", 4)
                    ssq = st_s[:, si:si + 1]; ms = st_s[:, 4 + si:5 + si]; rstd = st_s[:, 8 + si:9 + si]
                    fw.op("act", lambda e, k=k: e.activation(out=junk, in_=xt[k][:], func=AF.Square, accum_out=ssq), R=[B_xt[k]], W=[B_sqj[0], B_st[si]])
                    fw.op("pool", lambda e: e.tensor_scalar(out=ms, in0=ssq, scalar1=1.0 / DM, scalar2=EPS, op0=ALU.mult, op1=ALU.add), R=[B_st[si]], W=[B_st[si]])
                    fw.op("pool", lambda e: e.tensor_tensor(out=rstd, in0=ms, in1=dc[:, DC_MH:DC_MH + 1], op=ALU.pow), R=[B_st[si]], W=[B_st[si]])
                    fw.op("pool", lambda e, k=k: e.tensor_scalar(out=xs[k][:], in0=xt[k][:], scalar1=rstd, scalar2=0.0, op0=ALU.mult, op1=ALU.add), R=[B_xt[k], B_st[si]], W=[B_xs[k]])

                def Bp():
                    k = st["k"]
                    p, bp = getpT()
                    for kc in range(8):
                        fw.op("pe", lambda e, k=k, kc=kc, p=p: e.transpose(out=p[:, kc * 128:(kc + 1) * 128], in_=xs[k][:, kc * 128:(kc + 1) * 128], identity=ident[:]),
                              R=[B_xs[k]], W=[bp])
                    fw.op("dve", lambda e, p=p: e.tensor_copy(out=dest3[:, :, dest_cols], in_=p[:].rearrange("p (a b) -> p a b", a=8)), R=[bp], W=list(B_dest))
                return A, Bp

            def tm_norm_job(mmfn, nh, dests):
                hd = 512 // nh
                st = {}

                def A():
                    u, bu = mmfn()
                    i = rr("nst", NQS)
                    st["i"] = i
                    ssq = nst[:, i, 0:nh]; sd = nst[:, i, 4:4 + nh]; rs = nst[:, i, 8:8 + nh]
                    j = 0
                    fw.op("act", lambda e: e.activation(out=sqj[j][:], in_=u[:], func=AF.Square), R=[bu], W=[B_sqj[j]])
                    fw.op("dve", lambda e: e.tensor_reduce(out=ssq, in_=sqj[j][:].rearrange("p (a b) -> p a b", a=nh), axis=AX.X, op=ALU.add), R=[B_sqj[j]], W=[B_nst[i]])
                    fw.op("act", lambda e: e.activation(out=sd, in_=ssq, func=AF.Sqrt, scale=1.0 / hd, bias=dc[:, DC_EPS:DC_EPS + 1]), R=[B_nst[i]], W=[B_nst[i]])
                    fw.op("dve", lambda e: e.reciprocal(out=rs, in_=sd), R=[B_nst[i]], W=[B_nst[i]])
                    fw.op("dve", lambda e: e.tensor_tensor(out=qs[i][:].rearrange("p (a b) -> p a b", a=nh), in0=u[:].rearrange("p (a b) -> p a b", a=nh),
                                                           in1=rs.unsqueeze(2).to_broadcast([128, nh, hd]), op=ALU.mult), R=[bu, B_nst[i]], W=[B_qs[i]])

                def Bp():
                    i = st["i"]
                    p, bp = getpT()
                    for c in range(4):
                        fw.op("pe", lambda e, c=c, p=p: e.transpose(out=p[:, c * 128:(c + 1) * 128], in_=qs[i][:, c * 128:(c + 1) * 128], identity=ident[:]), R=[B_qs[i]], W=[bp])
                    for (dap, srcsel, gain, bds) in dests:
                        fw.op("act", lambda e, dap=dap, srcsel=srcsel, gain=gain, p=p: e.activation(out=dap, in_=srcsel(p), func=AF.Copy, scale=gain), R=[bp], W=bds)
                return A, Bp

            def merge_half(b, hh, B_zsrc, hT, B_hT):
                k = next_unit(U["MG%d_%d" % (b, hh)])
                for c4 in range(4):
                    def A(c4=c4):
                        u, bu = mm_fm(k, c4, hT, [B_hT])
                        col = DC_HBM + b * 8 + hh * 4 + c4
                        fw.op("act", lambda e: e.activation(out=gt[:, c4, :], in_=u[:], func=AF.Tanh, scale=0.5, bias=dc[:, col:col + 1]), R=[bu], W=[B_gt[c4]])
                    push(A)
                    if c4 == 1 and stt.extra:
                        stt.extra.pop(0)()
                release(k)
                k2 = next_unit(U["P%d_%d" % (b, hh)])
                for c4 in range(4):
                    def A(c4=c4):
                        c = hh * 4 + c4
                        u, bu = mm_fm(k2, c4, zT, list(B_zsrc))
                        if b == 0:
                            fw.op("dve", lambda e: e.scalar_tensor_tensor(out=yacc[:, c, :], in0=gt[:, c4, :], scalar=1.0, in1=u[:], op0=ALU.add, op1=ALU.mult),
                                  R=[B_gt[c4], bu], W=[B_y[c]])
                        else:
                            m = 0
                            fw.op("dve", lambda e: e.scalar_tensor_tensor(out=tmpm[m][:], in0=gt[:, c4, :], scalar=1.0, in1=u[:], op0=ALU.add, op1=ALU.mult),
                                  R=[B_gt[c4], bu], W=[B_tmpm[m]])
                            if b == 1:
                                fw.op("pool", lambda e: e.tensor_tensor(out=yacc[:, c, :], in0=yacc[:, c, :], in1=tmpm[m][:], op=ALU.add), R=[B_y[c], B_tmpm[m]], W=[B_y[c]])
                            else:
                                fw.op("pool", lambda e: e.tensor_tensor(out=sg[:, c, :], in0=yacc[:, c, :], in1=tmpm[m][:], op=ALU.add), R=[B_y[c], B_tmpm[m]], W=[B_sg[c]])
                    push(A)
                    if c4 == 1 and stt.extra:
                        stt.extra.pop(0)()
                release(k2)

            def mem_prepass(seq):
                memT = zT
                for ms_ in range(2):
                    r0 = seq * MEMT + ms_ * 128
                    A, Bp = norm_rows_job(dr["mem"][r0:r0 + 128, :], memT, slice(ms_ * 128, (ms_ + 1) * 128), B_zT)
                    A(); Bp()
                for hh in range(2):
                    k = next_unit(U["MK%d" % hh])
                    for ms_ in range(2):
                        dests = []
                        for par in range(2):
                            dap = KmT[:, 4 * hh + par:4 * hh + par + 3:2, ms_ * 128:(ms_ + 1) * 128]
                            dests.append((dap, (lambda p, par=par: p[:].rearrange("p (a b) -> p a b", a=8)[:, par:par + 3:2, :]), cp[:, CP_MKG + par:CP_MKG + par + 1], [B_KmT]))
                        push(*tm_norm_job((lambda k=k, ms_=ms_: mm_tm(k, memT, ms_, list(B_zT))), 2, dests))
                    release(k)
                for hh in range(2):
                    k = next_unit(U["MV%d" % hh])
                    for ms_ in range(2):
                        def A(k=k, ms_=ms_, hh=hh):
                            u, bu = mm_tm(k, memT, ms_, list(B_zT))
                            fw.op("act", lambda e: e.activation(out=Vm[:, ms_, hh * 512:(hh + 1) * 512], in_=u[:], func=AF.Copy), R=[bu], W=[B_Vm])
                        push(A)
                    release(k)
                flush()
                fw.op("pool", lambda e: e.memset(hist[:], 0.0), W=B_hist)
                fw.op("pool", lambda e: e.memset(hst[:], 0.0), W=B_hst)

            def stage0_jobs(seq, t, hT, B_hT):
                row0 = seq * SEQ + t * TT
                return [norm_rows_job(dr["x"][row0 + s * 128:row0 + (s + 1) * 128, :], hT, slice(s * 128, (s + 1) * 128), [B_hT]) for s in range(4)]

            def tile(seq, t, gi, nxt):
                row0 = seq * SEQ + t * TT
                hT = hTs[gi % 2]; B_hT = B_hTs[gi % 2]

                for hh in range(2):
                    k = next_unit(U["GR%d" % hh])
                    for c4 in range(4):
                        n = hh * 4 + c4
                        u, bu = mm_fm(k, c4, hT, [B_hT])
                        fw.op("act", lambda e, u=u, n=n: e.activation(out=sg[:, n, :], in_=u[:], func=AF.Silu), R=[bu], W=[B_sg[n]])
                    release(k)

                def lru_front(k, c4, n):
                    i = n % 2
                    u, bu = mm_fm(k, c4, hT, [B_hT])
                    fw.op("act", lambda e, u=u, i=i: e.activation(out=xrbuf[i][:, 3:515], in_=u[:], func=AF.Copy), R=[bu], W=[B_xr[i]])
                    fw.op("pool", lambda e, i=i, n=n: e.tensor_copy(out=xrbuf[i][:, 0:3], in_=hist[:, n, 0:3]), R=[B_hist[n]], W=[B_xr[i]])
                    fw.op("pool", lambda e, i=i, n=n: e.tensor_copy(out=hist[:, n, 0:3], in_=xrbuf[i][:, 512:515]), R=[B_xr[i]], W=[B_hist[n]])
                    cw = lambda j: cp[:, CP_CW + n * 4 + j:CP_CW + n * 4 + j + 1]
                    fw.op("dve", lambda e, i=i, n=n: e.tensor_scalar(out=xc[i][:], in0=xrbuf[i][:, 0:512], scalar1=cw(0), scalar2=cp[:, CP_CB + n:CP_CB + n + 1], op0=ALU.mult, op1=ALU.add),
                          R=[B_xr[i]], W=[B_xc[i]])
                    for j in range(1, 4):
                        fw.op("dve", lambda e, i=i, j=j: e.scalar_tensor_tensor(out=xc[i][:], in0=xrbuf[i][:, j:j + 512], scalar=cw(j), in1=xc[i][:], op0=ALU.mult, op1=ALU.add),
                              R=[B_xr[i], B_xc[i]], W=[B_xc[i]])
                    fw.op("pool", lambda e, i=i: e.tensor_scalar(out=xcb[i][:], in0=xc[i][:], scalar1=1.0, scalar2=0.0, op0=ALU.mult, op1=ALU.add), R=[B_xc[i]], W=[B_xcb[i]])

                def lru_back1(n):
                    i = n % 2
                    u1, bu1 = getU()
                    fw.op("pe", lambda e, u1=u1, i=i, n=n: e.matmul(u1[:], lhsT=wab[:, n, :], rhs=xcb[i][:], start=True, stop=True), R=[B_xcb[i]], W=[bu1])
                    u2, bu2 = getU()
                    fw.op("pe", lambda e, u2=u2, i=i, n=n: e.matmul(u2[:], lhsT=wxb[:, n, :], rhs=xcb[i][:], start=True, stop=True), R=[B_xcb[i]], W=[bu2])
                    fw.op("act", lambda e, u1=u1, i=i, n=n: e.activation(out=tr[i][:], in_=u1[:], func=AF.Tanh, scale=0.5, bias=dc[:, DC_HBA + n:DC_HBA + n + 1]), R=[bu1], W=[B_tr[i]])
                    fw.op("act", lambda e, u2=u2, i=i, n=n: e.activation(out=ti[i][:], in_=u2[:], func=AF.Tanh, scale=0.5, bias=dc[:, DC_HBX + n:DC_HBX + n + 1]), R=[bu2], W=[B_ti[i]])
                    fw.op("act", lambda e, i=i, n=n: e.activation(out=tr[i][:], in_=tr[i][:], func=AF.Exp, scale=dc[:, DC_CH + n:DC_CH + n + 1], bias=dc[:, DC_CH + n:DC_CH + n + 1]), R=[B_tr[i]], W=[B_tr[i]])
                    fw.op("pool", lambda e, i=i: e.tensor_tensor(out=a2[i][:], in0=tr[i][:], in1=tr[i][:], op=ALU.mult), R=[B_tr[i]], W=[B_a2[i]])

                def lru_back2(n):
                    i = n % 2
                    fw.op("act", lambda e, i=i: e.activation(out=a2[i][:], in_=a2[i][:], func=AF.Sqrt, scale=-1.0, bias=1.0), R=[B_a2[i]], W=[B_a2[i]])
                    fw.op("dve", lambda e, i=i: e.scalar_tensor_tensor(out=ti[i][:], in0=ti[i][:], scalar=1.0, in1=xc[i][:], op0=ALU.add, op1=ALU.mult), R=[B_ti[i], B_xc[i]], W=[B_ti[i]])
                    fw.op("dve", lambda e, i=i: e.scalar_tensor_tensor(out=ti[i][:], in0=ti[i][:], scalar=0.5, in1=a2[i][:], op0=ALU.mult, op1=ALU.mult), R=[B_ti[i], B_a2[i]], W=[B_ti[i]])
                    fw.op("dve", lambda e, i=i, n=n: e.tensor_tensor_scan(out=hb[i][:], data0=tr[i][:], data1=ti[i][:], initial=hst[:, n:n + 1], op0=ALU.mult, op1=ALU.add),
                          R=[B_tr[i], B_ti[i], B_hst[n]], W=[B_hb[i]])
                    fw.op("pool", lambda e, i=i, n=n: e.tensor_copy(out=hst[:, n:n + 1], in_=hb[i][:, 511:512]), R=[B_hb[i]], W=[B_hst[n]])
                    fw.op("pool", lambda e, i=i, n=n: e.tensor_tensor(out=zT[:, n, :], in0=hb[i][:], in1=sg[:, n, :], op=ALU.mult), R=[B_hb[i], B_sg[n]], W=[B_zT[n]])

                def unit_Q(nm, hh, hooks={}):
                    k = next_unit(U["%s%d" % (nm, hh)])
                    for s in range(4):
                        if nm == "Q":
                            dap = QT[:, 4 * hh:4 * hh + 4, s * 128:(s + 1) * 128]
                            bds = [B_QT[4 * hh + i] for i in range(4)]
                            gain = cp[:, CP_QG:CP_QG + 1]
                        else:
                            slot = (4 * t + s) % 8
                            dap = KT[:, 4 * hh:4 * hh + 4, slot * 128:(slot + 1) * 128]
                            bds = [B_KT[slot]]
                            gain = cp[:, CP_KG:CP_KG + 1]
                        push(*tm_norm_job((lambda k=k, s=s: mm_tm(k, hT, s, [B_hT])), 4,
                                          [(dap, (lambda p: p[:].rearrange("p (a b) -> p a b", a=8)[:, 0:4, :]), gain, bds)]))
                        if s in hooks:
                            hooks[s]()
                    release(k)

                def unit_V(hh, hooks={}):
                    k = next_unit(U["V%d" % hh])
                    for s in range(4):
                        def A(k=k, s=s, hh=hh):
                            slot = (4 * t + s) % 8
                            u, bu = mm_tm(k, hT, s, [B_hT])
                            if s % 2 == 0:
                                fw.op("act", lambda e: e.activation(out=Vr[:, slot, hh * 512:(hh + 1) * 512], in_=u[:], func=AF.Copy), R=[bu], W=[B_Vr[slot]])
                            else:
                                fw.op("dve", lambda e: e.tensor_copy(out=Vr[:, slot, hh * 512:(hh + 1) * 512], in_=u[:]), R=[bu], W=[B_Vr[slot]])
                        push(A)
                        if s in hooks:
                            hooks[s]()
                    release(k)

                def unit_GA(hh, hooks={}):
                    k = next_unit(U["GA%d" % hh])
                    for c4 in range(4):
                        def A(k=k, c4=c4, hh=hh):
                            n = hh * 4 + c4
                            u, bu = mm_fm(k, c4, hT, [B_hT])
                            fw.op("act", lambda e: e.activation(out=sg[:, n, :], in_=u[:], func=AF.Silu), R=[bu], W=[B_sg[n]])
                        push(A)
                        if c4 in hooks:
                            hooks[c4]()
                    release(k)

                others = [lambda hk: unit_Q("Q", 0, hk), lambda hk: unit_Q("Q", 1, hk), lambda hk: unit_Q("K", 0, hk), lambda hk: unit_Q("K", 1, hk),
                          lambda hk: unit_V(0, hk), lambda hk: unit_V(1, hk), lambda hk: unit_GA(0, hk), None]
                kx = None
                for n in range(8):
                    if n % 4 == 0:
                        kx = next_unit(U["XR%d" % (n // 4)])
                    lru_front(kx, n % 4, n)
                    if n % 4 == 3:
                        release(kx)
                    hk = {}
                    if n > 0:
                        hk = {0: (lambda n=n: lru_back1(n - 1)), 1: (lambda n=n: lru_back2(n - 1))}
                    if others[n] is not None:
                        others[n](hk)
                    else:
                        for f in hk.values():
                            f()
                lru_back1(7)
                lru_back2(7)
                unit_GA(1)
                flush()
                for hh in range(2):
                    merge_half(0, hh, B_zT, hT, B_hT)
                flush()

                scale = 128.0 ** -0.5

                def att_scores(j, h):
                    J = 4 * t + j
                    nd = min(J, 4) + 1
                    for d in range(nd):
                        slot = (J - d) % 8
                        fw.op("pe", lambda e, d=d, slot=slot, h=h, j=j: e.matmul(S[:, d * 128:(d + 1) * 128], lhsT=KT[:, h, slot * 128:(slot + 1) * 128],
                                                                                 rhs=QT[:, h, j * 128:(j + 1) * 128], start=True, stop=True),
                              R=[B_KT[slot], B_QT[h]], W=[B_S])
                    i = rr("Pe")
                    fw.op("act", lambda e, i=i, nd=nd: e.activation(out=Pe[i][:, 0:nd * 128], in_=S[:, 0:nd * 128], func=AF.Exp, scale=scale), R=[B_S], W=[B_Pe[i]])
                    fw.op("dve", lambda e, i=i, nd=nd, h=h: e.tensor_tensor(out=PT[i][:, 0:nd * 128], in0=Pe[i][:, 0:nd * 128], in1=biasR[:, h, 0:nd * 128], op=ALU.mult),
                          R=[B_Pe[i]], W=[B_PT[i]])
                    return i, nd

                def att_pv(j, h, i, nd):
                    J = 4 * t + j
                    od, bod = getU()
                    for d in range(nd):
                        slot = (J - d) % 8
                        fw.op("pe", lambda e, d=d, slot=slot, h=h, od=od, i=i: e.matmul(od[:, 0:128], lhsT=Vr[:, slot, h * 128:(h + 1) * 128],
                                                                                       rhs=PT[i][:, d * 128:(d + 1) * 128], start=(d == 0), stop=(d == nd - 1)),
                              R=[B_Vr[slot], B_PT[i]], W=[bod])
                    for d in range(nd):
                        fw.op("pe", lambda e, d=d, od=od, i=i: e.matmul(od[:, 128:256], lhsT=ones[:], rhs=PT[i][:, d * 128:(d + 1) * 128], start=(d == 0), stop=(d == nd - 1)),
                              R=[B_PT[i]], W=[bod])
                    r = rr("rD")
                    fw.op("act", lambda e, od=od, r=r: e.activation(out=rD[r][:], in_=od[:, 128:256], func=AF.Ln), R=[bod], W=[B_rD[r]])
                    fw.op("act", lambda e, r=r: e.activation(out=rD[r][:], in_=rD[r][:], func=AF.Exp, scale=-1.0), R=[B_rD[r]], W=[B_rD[r]])
                    fw.op("dve", lambda e, od=od, r=r: e.tensor_tensor(out=zt[r][:], in0=od[:, 0:128], in1=rD[r][:], op=ALU.mult), R=[bod, B_rD[r]], W=[B_zt[r]])
                    fw.op("pool", lambda e, r=r, h=h, j=j: e.tensor_tensor(out=zT[:, h, j * 128:(j + 1) * 128], in0=zt[r][:], in1=sg[:, h, j * 128:(j + 1) * 128], op=ALU.mult),
                          R=[B_zt[r], B_sg[h]], W=[B_zT[h]])

                pend = None
                for j in range(4):
                    for h in range(8):
                        if j == 3 and h == 0 and nxt is not None:
                            jobs = stage0_jobs(nxt[0], nxt[1], hTs[(gi + 1) % 2], B_hTs[(gi + 1) % 2])
                            jobs[0][0](); jobs[1][0]()
                            stt.extra = [(lambda jobs=jobs: (jobs[0][1](), jobs[2][0]())), (lambda jobs=jobs: (jobs[1][1](), jobs[3][0]())),
                                         (lambda jobs=jobs: jobs[2][1]()), (lambda jobs=jobs: jobs[3][1]())]
                        i, nd = att_scores(j, h)
                        if pend is not None:
                            att_pv(*pend)
                        pend = (j, h, i, nd)
                att_pv(*pend)
                for hh in range(2):
                    merge_half(1, hh, B_zT, hT, B_hT)
                flush()
                assert not stt.extra

                for hh in range(2):
                    k = next_unit(U["QM%d" % hh])
                    for s in range(4):
                        dests = []
                        for par in range(2):
                            dap = QT[:, 4 * hh + par:4 * hh + par + 3:2, s * 128:(s + 1) * 128]
                            bds = [B_QT[4 * hh + par], B_QT[4 * hh + par + 2]]
                            dests.append((dap, (lambda p, par=par: p[:].rearrange("p (a b) -> p a b", a=8)[:, par:par + 3:2, :]), cp[:, CP_MQG + par:CP_MQG + par + 1], bds))
                        push(*tm_norm_job((lambda k=k, s=s: mm_tm(k, hT, s, [B_hT])), 2, dests))
                    release(k)
                for hh in range(2):
                    k = next_unit(U["GM%d" % hh])
                    for c4 in range(4):
                        def A(k=k, c4=c4, hh=hh):
                            n = hh * 4 + c4
                            u, bu = mm_fm(k, c4, hT, [B_hT])
                            fw.op("act", lambda e: e.activation(out=sg[:, n, :], in_=u[:], func=AF.Silu), R=[bu], W=[B_sg[n]])
                        push(A)
                    release(k)
                flush()
                mscale = 256.0 ** -0.5
                for hm in range(4):
                    for mc in range(2):
                        for dch in range(2):
                            c = 2 * hm + dch
                            fw.op("pe", lambda e, mc=mc, dch=dch, c=c: e.matmul(S[:, mc * 512:(mc + 1) * 512], lhsT=KmT[:, c, mc * 128:(mc + 1) * 128], rhs=QT[:, c, :],
                                                                                 start=(dch == 0), stop=(dch == 1)), R=[B_KmT, B_QT[c]], W=[B_S])
                    i = 0
                    fw.op("act", lambda e, i=i: e.activation(out=PmT[i][:].rearrange("p a b -> p (a b)"), in_=S[:], func=AF.Exp, scale=mscale), R=[B_S], W=[B_PmT[i]])
                    ud, bud = getU()
                    for mc in range(2):
                        fw.op("pe", lambda e, mc=mc, ud=ud, i=i: e.matmul(ud[:], lhsT=ones[:], rhs=PmT[i][:, mc, :], start=(mc == 0), stop=(mc == 1)), R=[B_PmT[i]], W=[bud])
                    fw.op("act", lambda e, ud=ud: e.activation(out=rDm[:], in_=ud[:], func=AF.Ln), R=[bud], W=[B_rDm])
                    fw.op("act", lambda e: e.activation(out=rDm[:], in_=rDm[:], func=AF.Exp, scale=-1.0), R=[B_rDm], W=[B_rDm])
                    for dch in range(2):
                        c = 2 * hm + dch
                        uo, buo = getU()
                        for mc in range(2):
                            fw.op("pe", lambda e, mc=mc, uo=uo, i=i, c=c: e.matmul(uo[:], lhsT=Vm[:, mc, c * 128:(c + 1) * 128], rhs=PmT[i][:, mc, :], start=(mc == 0), stop=(mc == 1)),
                                  R=[B_Vm, B_PmT[i]], W=[buo])
                        fw.op("dve", lambda e, uo=uo: e.tensor_tensor(out=ztm[:], in0=uo[:], in1=rDm[:], op=ALU.mult), R=[buo, B_rDm], W=[B_ztm])
                        fw.op("pool", lambda e, c=c: e.tensor_tensor(out=zT[:, c, :], in0=ztm[:], in1=sg[:, c, :], op=ALU.mult), R=[B_ztm, B_sg[c]], W=[B_zT[c]])
                xk = {}

                def x_reload(s):
                    k = rr("xt")
                    xk[s] = k
                    rows = slice(row0 + s * 128, row0 + (s + 1) * 128)
                    fw.dma(lambda e, k=k, rows=rows: e.dma_start(out=xt[k][:], in_=dr["x"][rows, :]), d_xt[k], W=[B_xt[k]])
                x_reload(0); x_reload(1)
                for hh in range(2):
                    merge_half(2, hh, B_zT, hT, B_hT)
                flush()

                k0 = next_unit(U["WO0"])
                k1 = next_unit(U["WO1"])
                for s in range(4):
                    if s >= 2:
                        x_reload(s)
                    k = xk[s]
                    rows = slice(row0 + s * 128, row0 + (s + 1) * 128)
                    for hh, kw in ((0, k0), (1, k1)):
                        u, bu = mm_tm(kw, sg, s, B_sg)
                        fw.op("dve", lambda e, u=u, k=k, hh=hh: e.tensor_tensor(out=xt[k][:, hh * 512:(hh + 1) * 512], in0=u[:], in1=xt[k][:, hh * 512:(hh + 1) * 512], op=ALU.add),
                              R=[bu, B_xt[k]], W=[B_xt[k]])
                    fw.dma(lambda e, k=k, rows=rows: e.dma_start(out=out[rows, :], in_=xt[k][:]), d_out[k], R=[B_xt[k]])
                release(k0)
                release(k1)

            tiles = [(sq_, t_) for sq_ in range(nseq) for t_ in range(nt)]
            for A, Bp in stage0_jobs(0, 0, hTs[0], B_hTs[0]):
                A(); Bp()
            for gi, (seq, t) in enumerate(tiles):
                if t == 0:
                    mem_prepass(seq)
                nxt = tiles[gi + 1] if gi + 1 < len(tiles) else None
                tile(seq, t, gi, nxt)
            fw.wait_all_dma("sp")
            fw.barrier()
            fw.emit()
    return nc


_NC_CACHE = {}


def _host_consts(inp):
    f = lambda a: np.asarray(a, dtype=np.float32)
    cpk = np.zeros((128, NCP), np.float32)
    cpk[:, CP_NG:CP_NG + 8] = f(inp["norm_g"])[0].reshape(8, 128).T
    cpk[:, CP_MNG:CP_MNG + 8] = f(inp["mem_norm_g"])[0].reshape(8, 128).T
    cw = f(inp["conv_w"])[0]
    cpk[:, CP_CW:CP_CW + 32] = cw.reshape(4, 8, 128).transpose(2, 1, 0).reshape(128, 32)
    cpk[:, CP_CB:CP_CB + 8] = f(inp["conv_b"])[0].reshape(8, 128).T
    cpk[:, CP_BA:CP_BA + 8] = f(inp["lru_ba"])[0].reshape(8, 128).T
    cpk[:, CP_BX:CP_BX + 8] = f(inp["lru_bx"])[0].reshape(8, 128).T
    cpk[:, CP_LAM:CP_LAM + 8] = f(inp["lru_lambda"])[0].reshape(8, 128).T
    cpk[:, CP_BM:CP_BM + 24] = f(inp["b_merge"])[0].reshape(24, 128).T
    cpk[:, CP_QG] = f(inp["q_norm_g"])[0]
    cpk[:, CP_KG] = f(inp["k_norm_g"])[0]
    cpk[:, CP_MQG:CP_MQG + 2] = f(inp["mem_q_norm_g"])[0].reshape(2, 128).T
    cpk[:, CP_MKG:CP_MKG + 2] = f(inp["mem_k_norm_g"])[0].reshape(2, 128).T
    rb = f(inp["rel_bias"])[0]
    kl = np.arange(128)[:, None]
    ii = np.arange(640)[None, :]
    idx = np.clip(ii - kl, -256, 256) + 256
    biasr = np.ascontiguousarray(rb[:, idx].transpose(1, 0, 2))
    return cpk, biasr


def kernel(**inp):
    if "nc" not in _NC_CACHE:
        _NC_CACHE["nc"] = build_program()
    nc = _NC_CACHE["nc"]
    f = lambda a: np.ascontiguousarray(np.asarray(a, dtype=np.float32))
    x = f(inp["x"])
    mem = f(inp["mem"])
    cpk, biasr = _host_consts(inp)
    shared = {
        "w_in": f(inp["w_in"])[0], "w_mem_kv": f(inp["w_mem_kv"])[0],
        "w_proj_rnn": f(inp["w_proj_rnn"])[0], "w_proj_att": f(inp["w_proj_att"])[0],
        "w_proj_mem": f(inp["w_proj_mem"])[0], "w_out": f(inp["w_out"])[0],
        "lru_wa": f(inp["lru_wa"])[0], "lru_wx": f(inp["lru_wx"])[0],
        "cpack": cpk, "biasr": biasr,
    }
    in_maps = []
    for c in range(N_CORES):
        m = dict(shared)
        m["x"] = x[NSEQ * c:NSEQ * (c + 1)].reshape(NSEQ * SEQ, DM)
        m["mem"] = mem[NSEQ * c:NSEQ * (c + 1)].reshape(NSEQ * MEMT, DM)
        in_maps.append(m)
    res = run_bass_kernel_spmd(nc, in_maps, core_ids=list(range(N_CORES)))
    outs = [np.asarray(r["out"], dtype=np.float32).reshape(NSEQ, SEQ, DM) for r in res.results]
    return np.concatenate(outs, axis=0)
```
